# Optimizing a Trainium2 kernel written in Bass

```python
import math
import jax, jax.numpy as jnp
from jax import lax
import numpy as np

D_MODEL = 1024
BATCH = 32
SEQ = 256
DEPTH = 1
DEC_BATCH = 4
DEC_SEQ = 4096
PAST_LEN = 512

GRID_W = 64
CHUNK = 128
N_GROUPS_A = 4
D_A = 512
GROUP_A = D_A // N_GROUPS_A
N_HEADS_B = 8
HEAD_DIM_B = 64
V_DIM_B = 2 * HEAD_DIM_B
D_B = N_HEADS_B * V_DIM_B
D_QK = N_HEADS_B * 2 * HEAD_DIM_B
D_FF = 2816
CONV_W = 3
ROPE_THETA = 10000.0
EPS = 1e-6
Q_BLOCK = 128
IN_SPLITS = (D_A, 2 * D_A, 2 * D_A + D_QK, 2 * D_A + 2 * D_QK, 2 * D_A + 2 * D_QK + D_B,
             2 * D_A + 2 * D_QK + D_B + D_MODEL)
D_IN = 2 * D_A + 2 * D_QK + D_B + 2 * D_MODEL

kernel_name = "hybrid_gmlp_diffattn_prefix_dit_step"


def lambda_init(layer_idx):
    return 0.8 - 0.6 * math.exp(-0.3 * layer_idx)


def rmsnorm(x, g):
    xf = x.astype(jnp.float32)
    r = lax.rsqrt(jnp.mean(xf * xf, axis=-1, keepdims=True) + EPS)
    return (xf * r).astype(x.dtype) * g


def axial_rope(n_tok):
    rows = n_tok // GRID_W
    rr, cc = jnp.meshgrid(jnp.arange(rows), jnp.arange(GRID_W), indexing="ij")
    row = rr.reshape(-1).astype(jnp.float32)
    col = cc.reshape(-1).astype(jnp.float32)
    n_freq = HEAD_DIM_B // 4
    inv = ROPE_THETA ** (-jnp.arange(n_freq, dtype=jnp.float32) / n_freq)
    ang = jnp.concatenate([row[:, None] * inv, col[:, None] * inv], axis=-1)
    return jnp.cos(ang), jnp.sin(ang)


def apply_rope(x, cos, sin):
    half = HEAD_DIM_B // 2
    c = cos[None, :, None, None, :].astype(x.dtype)
    s = sin[None, :, None, None, :].astype(x.dtype)
    x1, x2 = x[..., :half], x[..., half:]
    return jnp.concatenate([x1 * c - x2 * s, x2 * c + x1 * s], axis=-1)


def chunk_gmlp(u, v, w_s, b_s):
    B, T, _ = v.shape
    vc = v.reshape(B, T // CHUNK, CHUNK, N_GROUPS_A, GROUP_A)
    mixed = jnp.einsum("gpq,bnqgc->bnpgc", w_s, vc) + b_s.T[None, None, :, :, None]
    return u * mixed.reshape(B, T, D_A)


def diff_attention(q, k, v, lam):
    B, T, H, _, Dh = q.shape
    nb = T // Q_BLOCK
    scale = Dh ** -0.5
    qb = q.reshape(B, nb, Q_BLOCK, H, 2, Dh).transpose(1, 0, 2, 3, 4, 5)

    def one_block(qi):
        s = jnp.einsum("bqhmd,bkhmd->bhmqk", qi, k).astype(jnp.float32) * scale
        p = jax.nn.softmax(s, axis=-1)
        a = p[:, :, 0] - lam * p[:, :, 1]
        return jnp.einsum("bhqk,bkhe->bqhe", a.astype(v.dtype), v)

    o = lax.map(one_block, qb)
    return o.transpose(1, 0, 2, 3, 4).reshape(B, T, H, v.shape[-1])


def token_mixers(h, ctx_k, ctx_v, rope, p, lam_init):
    B, T, _ = h.shape
    proj = h @ p["w_in"]
    u, va, q, k, v, ga, gb = jnp.split(proj, IN_SPLITS, axis=-1)
    a = chunk_gmlp(jax.nn.gelu(u), rmsnorm(jax.nn.gelu(va), p["g_sgu"]), p["w_s"], p["b_s"])
    q = q.reshape(B, T, N_HEADS_B, 2, HEAD_DIM_B)
    k = k.reshape(B, T, N_HEADS_B, 2, HEAD_DIM_B)
    v = v.reshape(B, T, N_HEADS_B, V_DIM_B)
    k_own = k.reshape(B, T, N_HEADS_B, 2 * HEAD_DIM_B)
    if rope is not None:
        cos, sin = rope
        q = apply_rope(q, cos, sin)
        k = apply_rope(k, cos, sin)
    if ctx_k is not None:
        S = ctx_k.shape[1]
        k = jnp.concatenate([ctx_k.reshape(B, S, N_HEADS_B, 2, HEAD_DIM_B), k], axis=1)
        v_all = jnp.concatenate([ctx_v, v], axis=1)
    else:
        v_all = v
    lq1, lk1, lq2, lk2 = (p[n].astype(jnp.float32) for n in ("lam_q1", "lam_k1", "lam_q2", "lam_k2"))
    lam = jnp.exp(jnp.sum(lq1 * lk1)) - jnp.exp(jnp.sum(lq2 * lk2)) + lam_init
    o = diff_attention(q, k, v_all, lam)
    o = rmsnorm(o, p["g_subln"]) * (1.0 - lam_init)
    b = o.reshape(B, T, D_B)
    merged = jax.nn.sigmoid(ga) * (a @ p["w_a"]) + jax.nn.sigmoid(gb) * (b @ p["w_b"])
    return merged @ p["w_o"], k_own, v


def conv_ffn(h, p):
    a, g = jnp.split(h @ p["w_up"], 2, axis=-1)
    a = lax.conv_general_dilated(a, p["conv_w"][:, None, :], window_strides=(1,), padding=((CONV_W // 2, CONV_W // 2),),
                                 dimension_numbers=("NWC", "WIO", "NWC"), feature_group_count=D_FF) + p["conv_b"]
    return (jax.nn.gelu(a) * g) @ p["w_down"]


def trunk_layer(x, cond, ctx_k, ctx_v, rope, p, lam_init):
    mod = jax.nn.silu(cond) @ p["w_ada"] + p["b_ada"]
    sh1, sc1, g1, sh2, sc2, g2 = jnp.split(mod[:, None, :], 6, axis=-1)
    h = rmsnorm(x, p["g_pre_mix"]) * (1 + sc1) + sh1
    mix, k_own, v_own = token_mixers(h, ctx_k, ctx_v, rope, p, lam_init)
    x = x + g1 * rmsnorm(mix, p["g_post_mix"])
    h = rmsnorm(x, p["g_pre_ffn"]) * (1 + sc2) + sh2
    x = x + g2 * rmsnorm(conv_ffn(h, p), p["g_post_ffn"])
    return x, k_own, v_own


def setup_inputs(seed: int = 0) -> dict:
    key = jax.random.key(seed)
    ks = iter(jax.random.split(key, 32))
    nrm = lambda shape, s: jax.random.normal(next(ks), shape, jnp.float32) * s
    gain = lambda shape: 1.0 + nrm(shape, 0.05)
    L = DEPTH
    return {
        "x_prompt": nrm((BATCH, SEQ, D_MODEL), 1.0),
        "x_sample": nrm((DEC_BATCH, DEC_SEQ, D_MODEL), 1.0),
        "c": nrm((DEC_BATCH, D_MODEL), 1.0),
        "cache_k": nrm((DEC_BATCH, L, PAST_LEN, N_HEADS_B, 2 * HEAD_DIM_B), 1.0),
        "cache_v": nrm((DEC_BATCH, L, PAST_LEN, N_HEADS_B, V_DIM_B), 1.0),
        "c_ctx": nrm((D_MODEL,), 1.0),
        "w_ada": nrm((L, D_MODEL, 6 * D_MODEL), 0.5 * D_MODEL ** -0.5),
        "b_ada": nrm((L, 6 * D_MODEL), 0.02),
        "g_pre_mix": gain((L, D_MODEL)),
        "g_post_mix": gain((L, D_MODEL)),
        "g_pre_ffn": gain((L, D_MODEL)),
        "g_post_ffn": gain((L, D_MODEL)),
        "w_in": nrm((L, D_MODEL, D_IN), D_MODEL ** -0.5),
        "g_sgu": gain((L, D_A)),
        "w_s": nrm((L, N_GROUPS_A, CHUNK, CHUNK), CHUNK ** -0.5),
        "b_s": gain((L, N_GROUPS_A, CHUNK)),
        "lam_q1": nrm((L, HEAD_DIM_B), 0.1),
        "lam_k1": nrm((L, HEAD_DIM_B), 0.1),
        "lam_q2": nrm((L, HEAD_DIM_B), 0.1),
        "lam_k2": nrm((L, HEAD_DIM_B), 0.1),
        "g_subln": gain((L, V_DIM_B)),
        "w_a": nrm((L, D_A, D_MODEL), D_A ** -0.5),
        "w_b": nrm((L, D_B, D_MODEL), D_B ** -0.5),
        "w_o": nrm((L, D_MODEL, D_MODEL), D_MODEL ** -0.5),
        "w_up": nrm((L, D_MODEL, 2 * D_FF), D_MODEL ** -0.5),
        "conv_w": nrm((L, CONV_W, D_FF), CONV_W ** -0.5),
        "conv_b": nrm((L, D_FF), 0.02),
        "w_down": nrm((L, D_FF, D_MODEL), D_FF ** -0.5),
    }


def reference(x_prompt, x_sample, c, cache_k, cache_v, c_ctx, w_ada, b_ada, g_pre_mix, g_post_mix,
              g_pre_ffn, g_post_ffn, w_in, g_sgu, w_s, b_s, lam_q1, lam_k1, lam_q2, lam_k2, g_subln,
              w_a, w_b, w_o, w_up, conv_w, conv_b, w_down):
    rope = axial_rope(x_sample.shape[1])
    yp, ys = x_prompt, x_sample
    new_k, new_v = [], []
    for l in range(DEPTH):
        p = {"w_ada": w_ada[l], "b_ada": b_ada[l], "g_pre_mix": g_pre_mix[l], "g_post_mix": g_post_mix[l],
             "g_pre_ffn": g_pre_ffn[l], "g_post_ffn": g_post_ffn[l], "w_in": w_in[l], "g_sgu": g_sgu[l],
             "w_s": w_s[l], "b_s": b_s[l], "lam_q1": lam_q1[l], "lam_k1": lam_k1[l], "lam_q2": lam_q2[l],
             "lam_k2": lam_k2[l], "g_subln": g_subln[l], "w_a": w_a[l], "w_b": w_b[l], "w_o": w_o[l],
             "w_up": w_up[l], "conv_w": conv_w[l], "conv_b": conv_b[l], "w_down": w_down[l]}
        lam_init = lambda_init(l)
        yp, k_ctx, v_ctx = trunk_layer(yp, c_ctx[None, :], None, None, None, p, lam_init)
        new_k.append(k_ctx)
        new_v.append(v_ctx)
        ys, _, _ = trunk_layer(ys, c, cache_k[:, l], cache_v[:, l], rope, p, lam_init)
    new_k_arr = jnp.stack(new_k, axis=1)
    new_v_arr = jnp.stack(new_v, axis=1)
    return (yp, ys, new_k_arr, new_v_arr)
```

```python
import os
import math
from contextlib import ExitStack
import numpy as np
import ml_dtypes
import concourse.bass as bass
import concourse.mybir as mybir
from concourse.bass_utils import run_bass_kernel_spmd

F32 = mybir.dt.float32
BF16 = mybir.dt.bfloat16
AF = mybir.ActivationFunctionType
ALU = mybir.AluOpType
AX = mybir.AxisListType

D = 1024
T = 256
NT = T // 128
NH = 8
DFF = 2816
NFC = DFF // 128
EPS = 1e-6
NKEY = 4608
NKT = NKEY // 128
LAM_INIT = 0.8 - 0.6 * math.exp(0.0)
GC1 = 0.7978845608028654
GC0 = math.sqrt(0.044715)
ENGS = ["pe", "act", "dve", "pool", "sp"]

NP_TILES = int(os.environ.get("MK_NP", "4"))
NS_TILES = int(os.environ.get("MK_NS", "8"))
DO_SAMPLE = os.environ.get("MK_SAMPLE", "1") == "1"


class _Op:
    __slots__ = ("eng", "fn", "deps", "need_inc", "inc_val", "dma", "sem")


class Sched:
    def __init__(self, nc):
        self.nc = nc
        self.ops = {e: [] for e in ENGS}
        self.last_w = {}
        self.readers = {}
        self.dma_keys = {}
        self.grouped = set()
        self.gi = 0
        self.limit = int(os.environ.get("MK_LIMIT", "1000000000"))

    def mark(self, name):
        if os.environ.get("MK_MARK"):
            print("MARK", name, self.gi, flush=True)

    def op(self, eng, fn, reads=(), writes=(), dma=None):
        self.gi += 1
        o = _Op()
        if self.gi > self.limit:
            o.eng, o.fn, o.dma, o.need_inc, o.inc_val, o.sem, o.deps = eng, fn, None, False, 0, None, []
            return o
        o.eng, o.fn, o.dma, o.need_inc, o.inc_val, o.sem = eng, fn, dma, dma is not None, 0, None
        deps = {}
        for k in reads:
            w = self.last_w.get(k)
            if w is not None:
                deps[id(w)] = w
            kn = k if isinstance(k, str) else k[0]
            if isinstance(kn, str) and len(kn) == 2 and kn[0] == "B" and kn[1].isdigit():
                for r in self.readers.get(k, ()):
                    if r.eng != eng:
                        deps[id(r)] = r
        for k in writes:
            w = self.last_w.get(k)
            if w is not None:
                deps[id(w)] = w
            for r in self.readers.get(k, ()):
                deps[id(r)] = r
        dl = []
        for d in deps.values():
            if d.eng == "pe" and eng == "pe" and d.dma is None and dma is None:
                continue
            d.need_inc = True
            dl.append(d)
        o.deps = dl
        for k in writes:
            self.last_w[k] = o
            self.readers[k] = []
        for k in reads:
            self.readers.setdefault(k, []).append(o)
        self.ops[eng].append(o)
        if dma is not None:
            self.dma_keys.setdefault(dma, 0)
        return o

    def emit(self, stack):
        nc = self.nc
        esem = {e: stack.enter_context(nc.semaphore("s_" + e)) for e in ENGS}
        dsem = {k: stack.enter_context(nc.semaphore("d_%d" % i)) for i, k in enumerate(self.dma_keys)}
        dcount = {k: 0 for k in self.dma_keys}
        for e in ENGS:
            c = 0
            for o in self.ops[e]:
                if o.dma is not None:
                    dcount[o.dma] += 16
                    o.inc_val = dcount[o.dma]
                    o.sem = dsem[o.dma]
                elif o.need_inc:
                    c += 1
                    o.inc_val = c
                    o.sem = esem[e]
        for e in ENGS:
            for o in self.ops[e]:
                if o.dma in self.grouped:
                    o.inc_val = dcount[o.dma]

        def run(e, engine):
            known = {}
            for o in self.ops[e]:
                need = {}
                for d in o.deps:
                    sid = id(d.sem)
                    if known.get(sid, 0) >= d.inc_val:
                        continue
                    if sid not in need or need[sid][1] < d.inc_val:
                        need[sid] = (d.sem, d.inc_val)
                for sid, (sem, val) in need.items():
                    engine.wait_ge(sem, val)
                    known[sid] = val
                ins = o.fn(engine)
                if o.need_inc:
                    ins.then_inc(o.sem, 16 if o.dma is not None else 1)

        with nc.Block() as block:
            @block.tensor
            def _(eng):
                run("pe", eng)

            @block.scalar
            def _(eng):
                run("act", eng)

            @block.vector
            def _(eng):
                run("dve", eng)

            @block.gpsimd
            def _(eng):
                run("pool", eng)

            @block.sync
            def _(eng):
                run("sp", eng)


def build_program():
    nc = bass.Bass("TRN2", target_bir_lowering=False)
    S = Sched(nc)

    def din(name, shape, dt=F32):
        return nc.dram_tensor(name, list(shape), dt, kind="ExternalInput").ap()

    def dout(name, shape):
        return nc.dram_tensor(name, list(shape), F32, kind="ExternalOutput").ap()

    xp = din("xp", [4 * T, D])
    xs = din("xs", [4096, D])
    cvec = din("cvec", [128, 8, 2])
    ck = din("ck", [512, D])
    cv = din("cv", [512, D])
    w_ada = din("w_ada", [D, 6 * D])
    b_adaF = din("b_adaF", [128, 48])
    b_ada = din("b_ada", [6 * D])
    gpreF = din("gpreF", [128, 8, 2])
    gpost = din("gpost", [2, D])
    w_in = din("w_in", [D, 6 * D])
    gsguF = din("gsguF", [128, 4])
    w_sT = din("w_sT", [128, 4, 128])
    b_s = din("b_s", [512])
    lamv = din("lamv", [4, 64])
    gsub = din("gsub", [128])
    w_a = din("w_a", [512, D])
    w_b = din("w_b", [D, D])
    w_o = din("w_o", [D, D])
    w_up = din("w_up", [D, 2 * DFF])
    convF = din("convF", [128, NFC, 4])
    w_down = din("w_down", [DFF, D])
    ropeCt = din("ropeCt", [4096, 64])
    ropeSt = din("ropeSt", [4096, 64])
    flags = din("flags", [128, 2])
    ident_d = din("ident", [128, 128], BF16)

    yp = dout("yp", [4 * T, D])
    ys = dout("ys", [2048, D])
    nk = dout("nk", [4 * T, D])
    nv = dout("nv", [4 * T, D])

    groups = {}
    glist = []

    def add_group(key, srcs, nkc, ncols):
        groups[key] = (len(glist), nkc, ncols)
        glist.append((key, srcs, nkc, ncols))

    for g in range(12):
        add_group(("in", g), [(w_in[:, g * 512:(g + 1) * 512], 0)], 8, 512)
    for g in range(2):
        add_group(("wa", g), [(w_a[:, g * 512:(g + 1) * 512], 0)], 4, 512)
    for g in range(2):
        add_group(("wb", g), [(w_b[:, g * 512:(g + 1) * 512], 0)], 8, 512)
    for g in range(2):
        add_group(("wo", g), [(w_o[:, g * 512:(g + 1) * 512], 0)], 8, 512)
    for i in range(11):
        add_group(("upp", i), [(w_up[:, i * 256:(i + 1) * 256], 0), (w_up[:, DFF + i * 256:DFF + (i + 1) * 256], 256)], 8, 512)
    for q in range(6):
        nch = 4 if q < 5 else 2
        add_group(("dn", q), [(w_down[q * 512:q * 512 + nch * 128, :], 0)], nch, 1024)
    NG = len(glist)
    Wscr = nc.dram_tensor("Wscr", [NG, 128, 4096], BF16, kind="Internal").ap()
    Kscr = nc.dram_tensor("Kscr", [NH, 128, NKEY], BF16, kind="Internal").ap()
    Vscr = nc.dram_tensor("Vscr", [NH, 128, NKT, 130], BF16, kind="Internal").ap()

    A = nc.alloc_sbuf_tensor
    NRING = 4
    TE = T + 2
    ring = [A("ring%d" % i, [128, 4096], BF16) for i in range(NRING)]
    xsl = [A("xsl%d" % i, [128, NT, D], F32) for i in range(3)]
    KVW = NKEY + NKT * 130
    KVbuf = A("KVbuf", [128, 2 * KVW], BF16)
    KTh = [KVbuf[:, i * KVW:i * KVW + NKEY] for i in range(2)]
    Vh = [KVbuf[:, i * KVW + NKEY:(i + 1) * KVW].rearrange("p (k e) -> p k e", e=130) for i in range(2)]
    kloc = A("kloc", [128, NH, T], BF16)
    vloc = A("vloc", [128, NT, NH, 130], BF16)
    hT = A("hT", [128, 8, T], BF16)
    h2T = [A("h2T%d" % i, [128, 8, TE], BF16) for i in range(2)]
    xn = [A("xn%d" % i, [128, D], BF16) for i in range(2)]
    uT = A("uT", [128, 4, T], BF16)
    vg = A("vg", [128, NT, 512], BF16)
    qm = A("qm", [128, NH, 2, T], BF16)
    aT = A("aT", [128, 4, T], BF16)
    bT = A("bT", [128, 8, T], BF16)
    mT = A("mT", [128, 8, T], BF16)
    tga = A("tga", [128, 4, T], F32)
    tgb = A("tgb", [128, 4, T], F32)
    pT = [A("pT%d" % i, [128, 2, T], BF16) for i in range(4)]
    rCt = [A("rCt%d" % i, [128, NT, 64], F32) for i in range(2)]
    rSt = [A("rSt%d" % i, [128, NT, 64], F32) for i in range(2)]
    NTMP = 9
    tmps = [A("tmp%d" % i, [128, 512], F32) for i in range(NTMP)]
    junks = [A("junk%d" % i, [128, D], BF16) for i in range(2)]
    bo = [A("bo%d" % i, [128, NT, 128], BF16) for i in range(2)]
    G = [A("G%d" % i, [128, D], F32) for i in range(4)]
    bsb = A("bsb", [128, 4, 128], F32)
    zT = [A("zT%d" % i, [128, T], BF16) for i in range(4)]
    yout = [A("yout%d" % i, [128, D], F32) for i in range(2)]
    ident = A("identsb", [128, 128], BF16)
    ones_bf = A("ones_bf", [128, 128], BF16)
    epst = A("epst", [128, 4], F32)
    cv_sb = A("cv_sb", [128, 8, 2], F32)
    csl = A("csl", [128, 8, 2], BF16)
    cslf = A("cslf", [128, 8, 2], F32)
    crep = xsl[1][:, 1, :].bitcast(BF16).rearrange("p (k c m) -> p k c m", k=8, c=2)
    CREPK = ("x", 1, 1)
    modF = A("modF", [128, 48, 2], F32)
    badF = A("badF", [128, 48], F32)
    gpre = A("gpre", [128, 8, 2], F32)
    scA = A("scA", [128, 8, 2], F32)
    scB = A("scB", [128, 8, 2], F32)
    gsg = A("gsg", [128, 4], F32)
    wsT = A("wsT", [128, 4, 128], BF16)
    wsTf = tmps[7][:, :].rearrange("p (g q) -> p g q", g=4)
    lamt = tmps[6][:, 0:256].rearrange("p (a b) -> p a b", a=4)
    lams = A("lams", [128, 8], F32)
    gsb = A("gsb", [128, 128], F32)
    cw = A("cw", [128, NFC, 4], F32)
    flg = A("flg", [128, 2], F32)
    HL = A("HL", [128, 8, 16], BF16)
    HR = A("HR", [128, 8, 16], BF16)
    st = A("st", [128, 96], F32)

    B0f = nc.alloc_psum_tensor("B0", [128, 512], F32)
    B0 = B0f[:, :].bitcast(BF16).rearrange("p (a b) -> p a b", a=8)
    PB = [B0f] + [nc.alloc_psum_tensor("B%d" % i, [128, 512], F32) for i in range(1, 8)]
    B7b = PB[7][:, :].bitcast(BF16).rearrange("p (a b) -> p a b", a=8)
    TB = [(B0, "B0"), (B7b, "B7")]

    KLOC = [("kloc", h) for h in range(8)]

    def O(eng, meth, reads, writes, **kw):
        return S.op(eng, lambda e: getattr(e, meth)(**kw), reads, writes)

    def dma(eng, out, in_, reads, writes, key):
        return S.op(eng, lambda e: e.dma_start(out=out, in_=in_), reads, writes, dma=key)

    out_dmas = []
    cnt = {"ring": 0, "xn": 0, "pT": 0, "pm": 0, "ps": 0, "zT": 0, "yout": 0, "kv": 0, "rope": 0, "stg": 0, "tmp": 0, "jk": 0, "fa": 0}
    pmset = [[1, 2, 7, 3, 4, 5, 6]]

    def nxt(name, n):
        v = cnt[name] % n
        cnt[name] += 1
        return v

    def tmp():
        i = nxt("tmp", NTMP)
        return tmps[i], ("tmp", i)

    def jk():
        i = nxt("jk", 2)
        return junks[i], ("jk", i)

    converted = set()
    stgv = [KVbuf[:, i * KVW:i * KVW + 8192].bitcast(F32) for i in range(2)]
    cvt = {"i": 0}

    def convert_into(key, r):
        gi, nkc, ncols = groups[key]
        _, srcs, _, _ = glist[gi]
        si = nxt("stg", 2)
        n = nkc * ncols
        sv = stgv[si][:, 0:n].rearrange("p (k n) -> p k n", n=ncols)
        for (src, off) in srcs:
            wcol = src.shape[1]
            dma("sp", sv[:, :, off:off + wcol], src.rearrange("(k p) n -> p k n", p=128), [], [("KTh", si), ("Vh", si)], ("KTh", si, off))
        eng = ["act", "dve"][cvt["i"] % 2]
        cvt["i"] += 1
        if eng == "act":
            O("act", "activation", [("KTh", si), ("Vh", si)], [("w", r)], out=ring[r][:, 0:n], in_=stgv[si][:, 0:n], func=AF.Copy)
        else:
            O(eng, "tensor_copy", [("KTh", si), ("Vh", si)], [("w", r)], out=ring[r][:, 0:n], in_=stgv[si][:, 0:n])
        dma("pool", Wscr[gi, :, 0:n], ring[r][:, 0:n], [("w", r)], [("Wscr", gi)], ("wst", r))
        converted.add(key)

    def wload(key):
        gi, nkc, ncols = groups[key]
        r = nxt("ring", NRING)
        view = ring[r][:, 0:nkc * ncols].rearrange("p (k n) -> p k n", n=ncols)
        if key not in converted:
            convert_into(key, r)
        else:
            dma("sp", ring[r][:, 0:nkc * ncols], Wscr[gi, :, 0:nkc * ncols], [("Wscr", gi)], [("w", r)], ("w", r))
        return view, ("w", r)

    def pmbank():
        ps_ = pmset[0]
        i = ps_[nxt("pm", len(ps_))]
        return PB[i], "B%d" % i

    def phase0():
        cst = "cst"
        pmset[0] = [1, 2]
        dma("sp", ident[:], ident_d, [], ["ident"], cst)
        dma("sp", cv_sb[:], cvec, [], ["cv_sb"], cst)
        dma("sp", badF[:], b_adaF, [], ["badF"], cst)
        dma("sp", gpre[:], gpreF, [], ["gpre"], cst)
        dma("sp", gsg[:], gsguF, [], ["gsg"], cst)
        dma("sp", wsTf, w_sT, [], [("tmp", 7)], cst)
        dma("sp", bsb[:].rearrange("p g q -> p (g q)"), b_s.partition_broadcast(128), [], ["bsb"], cst)
        dma("sp", tmps[6][:, 0:256], lamv.rearrange("a b -> (a b)").partition_broadcast(128), [], [("tmp", 6)], cst)
        dma("sp", gsb[:], gsub.partition_broadcast(128), [], ["gsb"], cst)
        dma("sp", cw[:], convF, [], ["cw"], cst)
        dma("sp", flg[:], flags, [], ["flg"], cst)
        S.grouped.add(cst)
        O("pool", "memset", [], ["ones_bf"], ap=ones_bf[:], constant=1.0)
        O("pool", "memset", [], ["epst"], ap=epst[:, 0:1], constant=EPS)
        O("pool", "memset", [], ["epst"], ap=epst[:, 1:2], constant=4 * EPS)
        O("pool", "memset", [], [("qm", h_) for h_ in range(NH)], ap=qm[:], constant=0.0)
        O("pool", "memset", [], [("vloc", s_) for s_ in range(NT)], ap=vloc[:], constant=1.0)
        O("pool", "memset", [], ["HL", "HR"], ap=HL[:], constant=0.0)
        O("pool", "memset", [], ["HL", "HR"], ap=HR[:], constant=0.0)
        S.mark('p0_consts')
        O("act", "activation", ["cv_sb"], ["cslf"], out=cslf[:], in_=cv_sb[:], func=AF.Silu)
        O("dve", "tensor_copy", ["cslf"], ["csl"], out=csl[:], in_=cslf[:])
        for kc in range(8):
            for ci in range(2):
                O("dve", "tensor_scalar", ["cslf", "ones_bf"], [CREPK], out=crep[:, kc, ci, :], in0=ones_bf[:],
                  scalar1=cslf[:, kc, ci:ci + 1], scalar2=None, op0=ALU.mult)
        O("dve", "tensor_copy", [("tmp", 7)], ["wsT"], out=wsT[:], in_=wsTf)
        O("dve", "tensor_scalar", ["gsb"], ["gsb"], out=gsb[:], in0=gsb[:], scalar1=(1.0 - LAM_INIT) * 0.5, scalar2=None, op0=ALU.mult)
        O("dve", "tensor_tensor", [("tmp", 6)], [("tmp", 6)], out=lamt[:, 0, :], in0=lamt[:, 0, :], in1=lamt[:, 1, :], op=ALU.mult)
        O("dve", "tensor_tensor", [("tmp", 6)], [("tmp", 6)], out=lamt[:, 2, :], in0=lamt[:, 2, :], in1=lamt[:, 3, :], op=ALU.mult)
        O("dve", "reduce_sum", [("tmp", 6)], ["lams"], out=lams[:, 0:1], in_=lamt[:, 0, :], axis=AX.X)
        O("dve", "reduce_sum", [("tmp", 6)], ["lams"], out=lams[:, 1:2], in_=lamt[:, 2, :], axis=AX.X)
        O("act", "activation", ["lams"], ["lams"], out=lams[:, 2:4], in_=lams[:, 0:2], func=AF.Exp)
        O("dve", "tensor_tensor", ["lams"], ["lams"], out=lams[:, 4:5], in0=lams[:, 3:4], in1=lams[:, 2:3], op=ALU.subtract)
        O("dve", "tensor_scalar", ["lams"], ["neglam"], out=lams[:, 5:6], in0=lams[:, 4:5], scalar1=-LAM_INIT, scalar2=None, op0=ALU.add)

        S.mark('p0_lam')
        stg = stgv
        S.mark('p0_prepass')
        adag = {}
        for g in range(12):
            si = nxt("stg", 2)
            dma("sp", stg[si][:].rearrange("p (k n) -> p k n", n=512), w_ada[:, g * 512:(g + 1) * 512].rearrange("(k p) n -> p k n", p=128),
                [], [("KTh", si), ("Vh", si)], ("KTh", si, 0))
            r = nxt("ring", NRING)
            if g % 2:
                O("dve", "tensor_copy", [("KTh", si), ("Vh", si)], [("w", r)], out=ring[r][:], in_=stg[si][:])
            else:
                O("act", "activation", [("KTh", si), ("Vh", si)], [("w", r)], out=ring[r][:], in_=stg[si][:], func=AF.Copy)
            wv = ring[r][:].rearrange("p (k n) -> p k n", n=512)
            pmod = PB[7]
            for blk in range(4):
                j = g * 4 + blk
                for kc in range(8):
                    O("pe", "matmul", [("w", r), "csl"], ["B7"], out=pmod[:, 2 * j:2 * j + 2], lhsT=wv[:, kc, blk * 128:(blk + 1) * 128],
                      rhs=csl[:, kc, :], start=(kc == 0), stop=(kc == 7))
            if g in (4, 5, 10, 11):
                which = 0 if g < 6 else 1
                half = g % 2
                if half == 0:
                    base = 2048 if which == 0 else 5120
                    dma("sp", xsl[0][:, 0, :], b_ada[base:base + 1024].partition_broadcast(128), [], [("x", 0, 0)], ("x", 0, 0))
                    dma("sp", xsl[1][:, 0, :], gpost[which].partition_broadcast(128), [], [("x", 1, 0)], ("x", 1, 0))
                for ci2 in range(2):
                    pb, pk = pmbank()
                    for kc in range(8):
                        O("pe", "matmul", [("w", r), CREPK], [pk], out=pb[:, :], lhsT=crep[:, kc, ci2, :], rhs=wv[:, kc, :],
                          start=(kc == 0), stop=(kc == 7))
                    gdst = G[ci2 * 2 + which][:, half * 512:(half + 1) * 512]
                    O("dve", "tensor_tensor", [pk, ("x", 0, 0)], [("G", ci2 * 2 + which, half)], out=gdst, in0=pb[:, :],
                      in1=xsl[0][:, 0, half * 512:(half + 1) * 512], op=ALU.add)
                    O("dve", "tensor_tensor", [("G", ci2 * 2 + which, half), ("x", 1, 0)], [("G", ci2 * 2 + which, half)], out=gdst, in0=gdst,
                      in1=xsl[1][:, 0, half * 512:(half + 1) * 512], op=ALU.mult)
        S.mark('p0_ada')
        O("dve", "tensor_tensor", ["B7", "badF"], ["modF"], out=modF[:, :, 0], in0=PB[7][:, 0:96].rearrange("p (j c) -> p j c", c=2)[:, :, 0],
          in1=badF[:], op=ALU.add)
        O("dve", "tensor_tensor", ["B7", "badF"], ["modF"], out=modF[:, :, 1], in0=PB[7][:, 0:96].rearrange("p (j c) -> p j c", c=2)[:, :, 1],
          in1=badF[:], op=ALU.add)
        for ci2 in range(2):
            O("dve", "scalar_tensor_tensor", ["modF", "gpre"], ["scA"], out=scA[:, :, ci2], in0=modF[:, 8:16, ci2], scalar=1.0,
              in1=gpre[:, :, 0], op0=ALU.add, op1=ALU.mult)
            O("dve", "scalar_tensor_tensor", ["modF", "gpre"], ["scB"], out=scB[:, :, ci2], in0=modF[:, 32:40, ci2], scalar=1.0,
              in1=gpre[:, :, 1], op0=ALU.add, op1=ALU.mult)

    def rstd_from(ss_ap, n, eps_col, out_ap, rk, wkey):
        O("act", "activation", rk + ["epst"], [wkey + "_ln", wkey], out=out_ap, in_=ss_ap, func=AF.Ln, scale=1.0 / n, bias=epst[:, eps_col:eps_col + 1])
        O("act", "activation", [wkey + "_ln"], [wkey], out=out_ap, in_=out_ap, func=AF.Exp, scale=-0.5)

    def norm_stats(xslot, xi, nt, sbase):
        ids = []
        for s in range(nt):
            xk = ("x", xi, s)
            j, jkk = jk()
            O("act", "activation", [xk], [jkk, ("ss", sbase, s)], out=j[:], in_=xslot[:, s, :], func=AF.Square,
              accum_out=st[:, sbase + s:sbase + s + 1])
            rstd_from(st[:, sbase + s:sbase + s + 1], D, 0, st[:, sbase + 4 + s:sbase + 5 + s], [("ss", sbase, s)], "rs%d_%d" % (sbase, s))
            xi2 = nxt("xn", 2)
            O("dve", "tensor_scalar", [xk, "rs%d_%d" % (sbase, s)], [("xn", xi2)], out=xn[xi2][:], in0=xslot[:, s, :],
              scalar1=st[:, sbase + 4 + s:sbase + 5 + s], scalar2=None, op0=ALU.mult)
            ids.append(xi2)
        return ids

    def norm_trans(ids, scale_ap, bias_ap, dst, dkey, coff=0):
        for s, xi2 in enumerate(ids):
            for kc in range(8):
                O("pe", "transpose", [("xn", xi2), "ident"], ["B0"], out=B0[:, kc, :], in_=xn[xi2][:, kc * 128:(kc + 1) * 128], identity=ident[:])
            for kc in range(8):
                O("dve", "tensor_scalar", ["B0", "scA", "scB", "modF"], [(dkey, s)], out=dst[:, kc, coff + s * 128:coff + (s + 1) * 128], in0=B0[:, kc, :],
                  scalar1=scale_ap[:, kc:kc + 1], scalar2=bias_ap[:, kc:kc + 1], op0=ALU.mult, op1=ALU.add)

    def norm_T(xslot, xi, nt, scale_ap, bias_ap, dst, dkey, sbase, coff=0):
        for s in range(nt):
            ids = norm_stats_one(xslot, xi, s, sbase)
            norm_trans_one(ids, s, scale_ap, bias_ap, dst, dkey, coff)

    def norm_stats_one(xslot, xi, s, sbase):
        xk = ("x", xi, s)
        j, jkk = jk()
        O("act", "activation", [xk], [jkk, ("ss", sbase, s)], out=j[:], in_=xslot[:, s, :], func=AF.Square,
          accum_out=st[:, sbase + s:sbase + s + 1])
        rstd_from(st[:, sbase + s:sbase + s + 1], D, 0, st[:, sbase + 4 + s:sbase + 5 + s], [("ss", sbase, s)], "rs%d_%d" % (sbase, s))
        xi2 = nxt("xn", 2)
        O("dve", "tensor_scalar", [xk, "rs%d_%d" % (sbase, s)], [("xn", xi2)], out=xn[xi2][:], in0=xslot[:, s, :],
          scalar1=st[:, sbase + 4 + s:sbase + 5 + s], scalar2=None, op0=ALU.mult)
        return xi2

    def norm_trans_one(xi2, s, scale_ap, bias_ap, dst, dkey, coff=0):
        for kc in range(8):
            O("pe", "transpose", [("xn", xi2), "ident"], ["B0"], out=B0[:, kc, :], in_=xn[xi2][:, kc * 128:(kc + 1) * 128], identity=ident[:])
        for kc in range(8):
            O("dve", "tensor_scalar", ["B0", "scA", "scB", "modF"], [(dkey, s)], out=dst[:, kc, coff + s * 128:coff + (s + 1) * 128], in0=B0[:, kc, :],
              scalar1=scale_ap[:, kc:kc + 1], scalar2=bias_ap[:, kc:kc + 1], op0=ALU.mult, op1=ALU.add)

    def gelu2(src, skey, n, dst, dkeys):
        t0, k0 = tmp()
        t1, k1 = tmp()
        O("act", "activation", [skey], [k0], out=t0[:, 0:n], in_=src, func=AF.Square, scale=GC0)
        O("dve", "scalar_tensor_tensor", [k0, skey], [k1], out=t1[:, 0:n], in0=t0[:, 0:n], scalar=1.0, in1=src, op0=ALU.add, op1=ALU.mult)
        O("act", "activation", [k1], [k0], out=t0[:, 0:n], in_=t1[:, 0:n], func=AF.Tanh, scale=GC1)
        O("dve", "scalar_tensor_tensor", [k0, skey], dkeys, out=dst, in0=t0[:, 0:n], scalar=1.0, in1=src, op0=ALU.add, op1=ALU.mult)

    def rope_tok(pb, pk, ri, s_, out_bf, okey):
        t4, k4 = tmp()
        t5, k5 = tmp()
        Ct = rCt[ri][:, s_, :]
        St = rSt[ri][:, s_, :].rearrange("p (h c) -> p h c", h=2)
        O("dve", "tensor_tensor", [pk, ("rC", ri)], [k4], out=t4[:, :].rearrange("p (a c) -> p a c", c=64),
          in0=pb.rearrange("p (a c) -> p a c", c=64), in1=Ct.unsqueeze(1).broadcast_to([128, 8, 64]), op=ALU.mult)
        pv4 = pb.rearrange("p (a h c) -> p a h c", h=2, c=32)
        t5v = t5[:, :].rearrange("p (a h c) -> p a h c", h=2, c=32)
        for half in range(2):
            O("dve", "tensor_tensor", [pk, ("rS", ri)], [(k5, half)], out=t5v[:, :, half, :], in0=pv4[:, :, 1 - half, :],
              in1=St[:, half, :].unsqueeze(1).broadcast_to([128, 8, 32]), op=ALU.mult)
        O("dve", "tensor_tensor", [k4, (k5, 0), (k5, 1)], [okey], out=out_bf, in0=t4[:, :], in1=t5[:, :], op=ALU.add)

    SB = [3, 4, 7]

    def attention_all(Tn, nt, sample, nkt):
        ctx = {}

        def head_ctx(h):
            if h not in ctx:
                if sample:
                    ki = nxt("kv", 2)
                    dma("sp", KTh[ki], Kscr[h], [("Kscr", h)], [("KTh", ki)], ("KTh", ki))
                    dma("sp", Vh[ki], Vscr[h], [("Vscr", h)], [("Vh", ki)], ("Vh", ki))
                    ctx[h] = (KTh[ki], [("KTh", ki)], (lambda kt, ki=ki: Vh[ki][:, kt, 0:129]), [("Vh", ki)])
                else:
                    ctx[h] = (kloc[:, h, :], [("kloc", h)], (lambda kt, h=h: vloc[:, kt, h, 0:129]), [("vloc", s_) for s_ in range(nt)])
            return ctx[h]

        def scores(u):
            h, kt = u
            KT_ap, kkeys, _, _ = head_ctx(h)
            si = SB[nxt("ps", 3)]
            ps, psk = PB[si], "B%d" % si
            for m in range(2):
                O("pe", "matmul", kkeys + [("qm", h)], [psk], out=ps[:, m * Tn:(m + 1) * Tn], lhsT=KT_ap[:, kt * 128:(kt + 1) * 128],
                  rhs=qm[:, h, m, 0:Tn], start=True, stop=True)
            return ps, psk

        units = [(h, kt) for h in range(NH) for kt in range(nkt)]
        head_ctx(0)
        pend = [scores(units[0]), scores(units[1])]
        stages = {1: 0, 4: 1, 6: 2, 8: 3, 10: 4, 13: 5}
        for i, (h, kt) in enumerate(units):
            if kt == 0 and h + 1 < NH:
                head_ctx(h + 1)
            _, _, V_of, vkeys = head_ctx(h)
            pbi = [5, 6] if h % 2 == 0 else [1, 2]
            ps, psk = pend.pop(0)
            if i + 2 < len(units):
                pend.append(scores(units[i + 2]))
            pi = nxt("pT", 4)
            O("act", "activation", [psk], [("pT", pi)], out=pT[pi][:, :, 0:Tn], in_=ps[:, 0:2 * Tn].rearrange("p (m t) -> p m t", m=2),
              func=AF.Exp, scale=0.125)
            for s in range(nt):
                for m in range(2):
                    O("pe", "matmul", vkeys + [("pT", pi)], ["B%d" % pbi[s]], out=PB[pbi[s]][:, m * 129:(m + 1) * 129],
                      lhsT=pT[pi][:, m, s * 128:(s + 1) * 128], rhs=V_of(kt), start=(kt == 0 and m == 0), stop=(kt == nkt - 1 and m == 1))
            if h >= 1:
                for kk, stg_ in stages.items():
                    if kt == min(kk, nkt - 1):
                        if stg_ == 5 and nkt <= 2:
                            if h >= 2:
                                attention_fin(h - 2, nt, 5)
                        else:
                            attention_fin(h - 1, nt, stg_)

    fin_state = {}

    def attention_fin(h, nt, stage):
        par = h % 2
        pbi = [5, 6] if par == 0 else [1, 2]
        if stage == 5:
            attention_fin_b(h, nt)
            return
        if stage == 0:
            fin_state[h] = (nxt("fa", 2), [tmp() for _ in range(nt)])
        fa, tws = fin_state[h]
        c0 = 16 + 16 * fa
        fk = ("fin", fa)
        if stage == 2:
            O("act", "activation", [(fk, "ss", s) for s in range(nt)] + ["epst"], [(fk, "ln")], out=st[:, c0 + 10:c0 + 10 + nt],
              in_=st[:, c0 + 8:c0 + 8 + nt], func=AF.Ln, scale=1.0 / 128, bias=epst[:, 0:1])
            return
        if stage == 3:
            O("act", "activation", [(fk, "ln")], [(fk, "rstd")], out=st[:, c0 + 12:c0 + 12 + nt], in_=st[:, c0 + 10:c0 + 10 + nt], func=AF.Exp, scale=-0.5)
            return
        for s in range(nt):
            pk = "B%d" % pbi[s]
            pov = PB[pbi[s]][:, 0:258].rearrange("p (m e) -> p m e", m=2)
            tw, twk = tws[s]
            cs = c0 + 4 * s
            if stage == 0:
                O("dve", "reciprocal", [pk], [(fk, 0, s)], out=st[:, cs:cs + 2], in_=pov[:, :, 128])
                O("dve", "tensor_tensor", [(fk, 0, s), "neglam"], [(fk, 1, s)], out=st[:, cs + 2:cs + 3], in0=st[:, cs + 1:cs + 2],
                  in1=lams[:, 5:6], op=ALU.mult)
                O("dve", "tensor_scalar", [pk, (fk, 0, s)], [twk], out=tw[:, 0:128], in0=pov[:, 0, 0:128],
                  scalar1=st[:, cs:cs + 1], scalar2=None, op0=ALU.mult)
                O("dve", "scalar_tensor_tensor", [pk, (fk, 1, s), twk], [(twk, "b")], out=tw[:, 128:256], in0=pov[:, 1, 0:128],
                  scalar=st[:, cs + 2:cs + 3], in1=tw[:, 0:128], op0=ALU.mult, op1=ALU.add)
            elif stage == 1:
                O("dve", "tensor_tensor", [(twk, "b")], [(twk, "c")], out=tw[:, 256:384], in0=tw[:, 128:256], in1=tw[:, 128:256], op=ALU.mult)
                O("dve", "reduce_sum", [(twk, "c")], [(fk, "ss", s)], out=st[:, c0 + 8 + s:c0 + 9 + s], in_=tw[:, 256:384], axis=AX.X)
            elif stage == 4:
                O("dve", "scalar_tensor_tensor", [(twk, "b"), (fk, "rstd"), "gsb"], [("bo", par, s)], out=bo[par][:, s, :],
                  in0=tw[:, 128:256], scalar=st[:, c0 + 12 + s:c0 + 13 + s], in1=gsb[:], op0=ALU.mult, op1=ALU.mult)

    def attention_fin_b(h, nt):
        par = h % 2
        for s in range(nt):
            O("pe", "transpose", [("bo", par, s), "ident"], ["B0"], out=B0[:, s, :], in_=bo[par][:, s, :], identity=ident[:])
        for s in range(nt):
            O("dve", "tensor_copy", ["B0"], [("bT", h)], out=bT[:, h, s * 128:(s + 1) * 128], in_=B0[:, s, :])

    def mixer_pre(t):
        Tn, nt, xi = t["T"], t["T"] // 128, t["xi"]
        xslot = xsl[xi]
        for s in range(nt):
            dma("sp", xslot[:, s, :], t["src"][s * 128:(s + 1) * 128, :], [], [("x", xi, s)], ("x", xi, s))
        if t["kind"] != "p":
            ri = nxt("rope", 2)
            dma("sp", rCt[ri][:, 0:nt, :], ropeCt[t["roff"]:t["roff"] + Tn, :].rearrange("(s p) c -> p s c", p=128), [], [("rC", ri)], ("rC", ri))
            dma("sp", rSt[ri][:, 0:nt, :], ropeSt[t["roff"]:t["roff"] + Tn, :].rearrange("(s p) c -> p s c", p=128), [], [("rS", ri)], ("rS", ri))
            t["ri"] = ri
        t["ids"] = norm_stats(xslot, xi, nt, 0)

    def mixer(t):
        Tn, nt, ci, xi = t["T"], t["T"] // 128, t["ci"], t["xi"]
        xslot = xsl[xi]
        sample = t["kind"] != "p"
        pmset[0] = [1, 2, 7, 3, 4, 5, 6]
        if sample:
            ri = t["ri"]
        norm_trans(t["ids"], scA[:, :, ci], modF[:, 0:8, ci], hT, "hT")
        hk = [("hT", s) for s in range(nt)]
        S.mark('m_norm1')
        wv, wkk = wload(("in", 0))
        for blk in range(4):
            pb, pk = pmbank()
            for kc in range(8):
                O("pe", "matmul", [wkk] + hk, [pk], out=pb[:, 0:Tn], lhsT=wv[:, kc, blk * 128:(blk + 1) * 128], rhs=hT[:, kc, 0:Tn],
                  start=(kc == 0), stop=(kc == 7))
            gelu2(pb[:, 0:Tn], pk, Tn, uT[:, blk, 0:Tn], [("uT", blk)])
        wv, wkk = wload(("in", 1))
        for s in range(nt):
            pb, pk = pmbank()
            for kc in range(8):
                O("pe", "matmul", [wkk, ("hT", s)], [pk], out=pb[:, :], lhsT=hT[:, kc, s * 128:(s + 1) * 128], rhs=wv[:, kc, :],
                  start=(kc == 0), stop=(kc == 7))
            tv, tvk = tmp()
            gelu2(pb[:, :], pk, 512, tv[:, :], [tvk])
            j, jkk = jk()
            O("act", "activation", [tvk], [jkk, ("ssv", s)], out=j[:, 0:512], in_=tv[:, :], func=AF.Square, accum_out=st[:, 8 + s:9 + s])
            rstd_from(st[:, 8 + s:9 + s], 512, 1, st[:, 10 + s:11 + s], [("ssv", s)], "rsv%d" % s)
            O("dve", "tensor_scalar", [tvk, "rsv%d" % s], [("vg", s)], out=vg[:, s, :], in0=tv[:, :], scalar1=st[:, 10 + s:11 + s], scalar2=None, op0=ALU.mult)
        qpend = []

        def q_flush():
            g_, s_, qb_, tqk_ = qpend.pop(0)
            tb_, tbk_ = TB[g_]
            for hh in range(4):
                O("pe", "transpose", [tqk_, "ident"], [tbk_], out=tb_[:, s_ * 4 + hh, :], in_=qb_[:, hh * 128:(hh + 1) * 128], identity=ident[:])
            O("act", "activation", [tbk_], [("qm", g_ * 4 + hh) for hh in range(4)], out=qm[0:64, g_ * 4:(g_ + 1) * 4, 0, s_ * 128:(s_ + 1) * 128],
              in_=tb_[0:64, s_ * 4:(s_ + 1) * 4, :], func=AF.Copy)
            O("dve", "tensor_copy", [tbk_], [("qm", g_ * 4 + hh) for hh in range(4)], out=qm[64:128, g_ * 4:(g_ + 1) * 4, 1, s_ * 128:(s_ + 1) * 128],
              in_=tb_[64:128, s_ * 4:(s_ + 1) * 4, :])

        for g in range(2):
            wv, wkk = wload(("in", 2 + g))
            if sample:
                pmset[0] = [1, 2, 3, 4, 5, 6]
                tb, tbk = TB[g]
                for s in range(nt):
                    pb, pk = pmbank()
                    for kc in range(8):
                        O("pe", "matmul", [wkk, ("hT", s)], [pk], out=pb[:, :], lhsT=hT[:, kc, s * 128:(s + 1) * 128], rhs=wv[:, kc, :],
                          start=(kc == 0), stop=(kc == 7))
                    tq, tqk = tmp()
                    qb = tq[:, 0:256].bitcast(BF16)
                    rope_tok(pb[:, :], pk, ri, s, qb, tqk)
                    qpend.append((g, s, qb, tqk))
                    if len(qpend) > 1:
                        q_flush()
            else:
                for blk in range(4):
                    h = g * 4 + blk
                    pb, pk = pmbank()
                    for kc in range(8):
                        O("pe", "matmul", [wkk] + hk, [pk], out=pb[:, 0:Tn], lhsT=wv[:, kc, blk * 128:(blk + 1) * 128], rhs=hT[:, kc, 0:Tn],
                          start=(kc == 0), stop=(kc == 7))
                    O("act", "activation", [pk], [("qm", h)], out=qm[0:64, h, 0, 0:Tn], in_=pb[0:64, 0:Tn], func=AF.Copy)
                    O("dve", "tensor_copy", [pk], [("qm", h)], out=qm[64:128, h, 1, 0:Tn], in_=pb[64:128, 0:Tn])
        while qpend:
            q_flush()
        pmset[0] = [1, 2, 7, 3, 4, 5, 6]
        if not sample:
            for g in range(2):
                wv, wkk = wload(("in", 4 + g))
                for blk in range(4):
                    h = g * 4 + blk
                    pb, pk = pmbank()
                    for kc in range(8):
                        O("pe", "matmul", [wkk] + hk, [pk], out=pb[:, 0:Tn], lhsT=wv[:, kc, blk * 128:(blk + 1) * 128], rhs=hT[:, kc, 0:Tn],
                          start=(kc == 0), stop=(kc == 7))
                    O("act" if blk % 2 else "dve", "activation" if blk % 2 else "tensor_copy", [pk], [("kloc", h)],
                      **(dict(out=kloc[:, h, 0:Tn], in_=pb[:, 0:Tn], func=AF.Copy) if blk % 2 else dict(out=kloc[:, h, 0:Tn], in_=pb[:, 0:Tn])))
                for s in range(nt):
                    pb, pk = pmbank()
                    for kc in range(8):
                        O("pe", "matmul", [wkk, ("hT", s)], [pk], out=pb[:, :], lhsT=hT[:, kc, s * 128:(s + 1) * 128], rhs=wv[:, kc, :],
                          start=(kc == 0), stop=(kc == 7))
                    O("act", "activation", [pk], [("yout", s, g)], out=yout[s][:, g * 512:(g + 1) * 512], in_=pb[:, :], func=AF.Copy)
            for s in range(nt):
                o = dma("pool", t["nk"][s * 128:(s + 1) * 128, :], yout[s][:], [("yout", s, 0), ("yout", s, 1)], [], ("yo", s))
                out_dmas.append(o)
            for g in range(2):
                wv, wkk = wload(("in", 6 + g))
                for s in range(nt):
                    pb, pk = pmbank()
                    for kc in range(8):
                        O("pe", "matmul", [wkk, ("hT", s)], [pk], out=pb[:, :], lhsT=hT[:, kc, s * 128:(s + 1) * 128], rhs=wv[:, kc, :],
                          start=(kc == 0), stop=(kc == 7))
                    O("act", "activation", [pk], [("yout", s, g)], out=yout[s][:, g * 512:(g + 1) * 512], in_=pb[:, :], func=AF.Copy)
                    O("dve", "tensor_copy", [pk], [("vloc", s)], out=vloc[:, s, g * 4:(g + 1) * 4, 0:128],
                      in_=pb[:, :].rearrange("p (h e) -> p h e", h=4))
            for s in range(nt):
                o = dma("pool", t["nv"][s * 128:(s + 1) * 128, :], yout[s][:], [("yout", s, 0), ("yout", s, 1)], [], ("yo", s))
                out_dmas.append(o)
        S.mark('m_kv')
        for g in range(4):
            pb, pk = pmbank()
            for s in range(nt):
                O("pe", "matmul", [("vg", s), "wsT"], [pk], out=pb[:, s * 128:(s + 1) * 128], lhsT=vg[:, s, g * 128:(g + 1) * 128], rhs=wsT[:, g, :],
                  start=True, stop=True)
            t4, k4 = tmp()
            for s in range(nt):
                O("dve", "scalar_tensor_tensor", [pk, "gsg", "bsb"], [(k4, s)], out=t4[:, s * 128:(s + 1) * 128], in0=pb[:, s * 128:(s + 1) * 128],
                  scalar=gsg[:, g:g + 1], in1=bsb[:, g, :], op0=ALU.mult, op1=ALU.add)
            O("dve", "scalar_tensor_tensor", [(k4, s) for s in range(nt)] + [("uT", g)], [("aT", g)], out=aT[:, g, 0:Tn], in0=t4[:, 0:Tn], scalar=0.25,
              in1=uT[:, g, 0:Tn], op0=ALU.mult, op1=ALU.mult)
        S.mark('m_gmlp')
        pmset[0] = [1]
        attention_all(Tn, nt, sample, NKT if sample else nt)
        for stg_ in range(5):
            attention_fin(NH - 1, nt, stg_)
        if not sample:
            attention_fin(NH - 2, nt, 5)
        attention_fin(NH - 1, nt, 5)
        S.mark('m_attn')
        pmset[0] = [1, 2, 7, 3, 4, 5, 6]
        bk = [("bT", h) for h in range(NH)]
        ak = [("aT", g) for g in range(4)]
        for g in range(2):
            wv, wkk = wload(("in", 8 + g))
            for blk in range(4):
                pb, pk = pmbank()
                for kc in range(8):
                    O("pe", "matmul", [wkk] + hk, [pk], out=pb[:, 0:Tn], lhsT=wv[:, kc, blk * 128:(blk + 1) * 128], rhs=hT[:, kc, 0:Tn],
                      start=(kc == 0), stop=(kc == 7))
                O("act", "activation", [pk], [("tga", blk)], out=tga[:, blk, 0:Tn], in_=pb[:, 0:Tn], func=AF.Tanh, scale=0.5)
            wv, wkk = wload(("in", 10 + g))
            for blk in range(4):
                pb, pk = pmbank()
                for kc in range(8):
                    O("pe", "matmul", [wkk] + hk, [pk], out=pb[:, 0:Tn], lhsT=wv[:, kc, blk * 128:(blk + 1) * 128], rhs=hT[:, kc, 0:Tn],
                      start=(kc == 0), stop=(kc == 7))
                O("act", "activation", [pk], [("tgb", blk)], out=tgb[:, blk, 0:Tn], in_=pb[:, 0:Tn], func=AF.Tanh, scale=0.5)
            wv, wkk = wload(("wa", g))
            for blk in range(4):
                pb, pk = pmbank()
                for kc in range(4):
                    O("pe", "matmul", [wkk] + ak, [pk], out=pb[:, 0:Tn], lhsT=wv[:, kc, blk * 128:(blk + 1) * 128], rhs=aT[:, kc, 0:Tn],
                      start=(kc == 0), stop=(kc == 3))
                O("dve", "scalar_tensor_tensor", [pk, ("tga", blk)], [("tga", blk)], out=tga[:, blk, 0:Tn], in0=tga[:, blk, 0:Tn], scalar=1.0,
                  in1=pb[:, 0:Tn], op0=ALU.add, op1=ALU.mult)
            wv, wkk = wload(("wb", g))
            for blk in range(4):
                pb, pk = pmbank()
                for kc in range(8):
                    O("pe", "matmul", [wkk] + bk, [pk], out=pb[:, 0:Tn], lhsT=wv[:, kc, blk * 128:(blk + 1) * 128], rhs=bT[:, kc, 0:Tn],
                      start=(kc == 0), stop=(kc == 7))
                O("dve", "scalar_tensor_tensor", [pk, ("tgb", blk)], [("tgb", blk)], out=tgb[:, blk, 0:Tn], in0=tgb[:, blk, 0:Tn], scalar=1.0,
                  in1=pb[:, 0:Tn], op0=ALU.add, op1=ALU.mult)
                O("dve", "tensor_tensor", [("tga", blk), ("tgb", blk)], [("mT", g * 4 + blk)], out=mT[:, g * 4 + blk, 0:Tn], in0=tga[:, blk, 0:Tn],
                  in1=tgb[:, blk, 0:Tn], op=ALU.add)
        S.mark('m_merge')
        mk = [("mT", j) for j in range(8)]
        for g in range(2):
            wv, wkk = wload(("wo", g))
            for s in range(nt):
                bi = 3 + 2 * s + g
                for kc in range(8):
                    O("pe", "matmul", [wkk] + mk, ["B%d" % bi], out=PB[bi][:, :], lhsT=mT[:, kc, s * 128:(s + 1) * 128], rhs=wv[:, kc, :],
                      start=(kc == 0), stop=(kc == 7))
        Gk = [("G", ci * 2 + 0, 0), ("G", ci * 2 + 0, 1)]
        for s in range(nt):
            post_norm_residual(s, xslot, xi, G[ci * 2 + 0], Gk, xslot[:, s, :], [("x", xi, s)], 48)
        S.mark('m_wo')
        hi = t["hi"]
        norm_T(xslot, xi, nt, scB[:, :, ci], modF[:, 24:32, ci], h2T[hi], ("h2T", hi), 64)
        tid = t["id"]
        O("pool", "tensor_copy", [(("h2T", hi), nt - 1)], ["HL"], out=HL[:, :, tid:tid + 1], in_=h2T[hi][:, :, Tn - 1:Tn])
        O("pool", "tensor_copy", [(("h2T", hi), 0)], ["HR"], out=HR[:, :, tid:tid + 1], in_=h2T[hi][:, :, 0:1])

    def post_norm_residual(s, xslot, xi, Gt, Gk, dst, dkeys, sbase):
        b0 = 3 + 2 * s
        c = sbase + 4 * s
        for g in range(2):
            j, jkk = jk()
            O("act", "activation", ["B%d" % (b0 + g)], [jkk, ("pss", c, g)], out=j[:, 0:512], in_=PB[b0 + g][:, :], func=AF.Square,
              accum_out=st[:, c + g:c + g + 1])
        O("dve", "tensor_tensor", [("pss", c, 0), ("pss", c, 1)], [("pss", c, 2)], out=st[:, c + 2:c + 3], in0=st[:, c:c + 1], in1=st[:, c + 1:c + 2],
          op=ALU.add)
        rstd_from(st[:, c + 2:c + 3], D, 0, st[:, c + 3:c + 4], [("pss", c, 2)], "pr%d" % c)
        for g in range(2):
            t4, k4 = tmp()
            O("dve", "scalar_tensor_tensor", ["B%d" % (b0 + g), "pr%d" % c, Gk[g]], [k4], out=t4[:, :], in0=PB[b0 + g][:, :],
              scalar=st[:, c + 3:c + 4], in1=Gt[:, g * 512:(g + 1) * 512], op0=ALU.mult, op1=ALU.mult)
            O("dve", "tensor_tensor", [k4, ("x", xi, s)], [dkeys[g if len(dkeys) > 1 else 0]],
              out=dst[:, g * 512:(g + 1) * 512], in0=t4[:, :], in1=xslot[:, s, g * 512:(g + 1) * 512], op=ALU.add)

    def ffn(t, hl_src, hr_src):
        Tn, nt, ci, xi, hi = t["T"], t["T"] // 128, t["ci"], t["xi"], t["hi"]
        xslot = xsl[xi]
        h2 = h2T[hi]
        hkey = ("h2T", hi)
        h2k = [(hkey, s) for s in range(nt)]
        halo = not (hl_src is None and hr_src is None)
        if halo:
            for col, src, hk_ in ((Tn, hl_src, (hkey, "hl")), (Tn + 1, hr_src, (hkey, "hr"))):
                if src is None:
                    O("pool", "memset", [], [hk_], ap=h2[:, :, col:col + 1], constant=0.0)
                else:
                    ap, keys, fc = src
                    if fc is None:
                        O("pool", "tensor_copy", keys, [hk_], out=h2[:, :, col:col + 1], in_=ap)
                    else:
                        O("pool", "tensor_scalar", keys + ["flg"], [hk_], out=h2[:, :, col:col + 1], in0=ap, scalar1=flg[:, fc:fc + 1], scalar2=None,
                          op0=ALU.mult)
        pbank = [1, 2]
        hbank = [7, 0]
        st_ = {"wu": None, "wdn": None}

        def ag(c):
            if c % 2 == 0:
                st_["wu"] = wload(("upp", c // 2))
            wu_v, wu_k = st_["wu"]
            ao = (c % 2) * 128
            go = 256 + (c % 2) * 128
            pb, pk = PB[pbank[c % 2]], "B%d" % pbank[c % 2]
            for kc in range(8):
                O("pe", "matmul", [wu_k] + h2k, [pk], out=pb[:, 0:Tn], lhsT=wu_v[:, kc, ao:ao + 128], rhs=h2[:, kc, 0:Tn], start=(kc == 0), stop=(kc == 7))
            if halo:
                hb, hbk = PB[hbank[c % 2]], "B%d" % hbank[c % 2]
                for kc in range(8):
                    O("pe", "matmul", [wu_k, (hkey, "hl"), (hkey, "hr")], [hbk], out=hb[:, 0:2], lhsT=wu_v[:, kc, ao:ao + 128], rhs=h2[:, kc, Tn:Tn + 2],
                      start=(kc == 0), stop=(kc == 7))
            for kc in range(8):
                O("pe", "matmul", [wu_k] + h2k, [pk], out=pb[:, Tn:2 * Tn], lhsT=wu_v[:, kc, go:go + 128], rhs=h2[:, kc, 0:Tn], start=(kc == 0), stop=(kc == 7))

        ag(0)
        for c in range(NFC):
            if c + 1 < NFC:
                ag(c + 1)
            pb, pk = PB[pbank[c % 2]], "B%d" % pbank[c % 2]
            hb, hbk = PB[hbank[c % 2]], "B%d" % hbank[c % 2]
            ac, ack = tmp()
            O("dve", "tensor_scalar", [pk, "cw"], [ack], out=ac[:, 0:Tn], in0=pb[:, 0:Tn], scalar1=cw[:, c, 1:2], scalar2=cw[:, c, 3:4], op0=ALU.mult, op1=ALU.add)
            O("dve", "scalar_tensor_tensor", [pk, "cw", ack], [ack], out=ac[:, 1:Tn], in0=pb[:, 0:Tn - 1], scalar=cw[:, c, 0:1], in1=ac[:, 1:Tn],
              op0=ALU.mult, op1=ALU.add)
            O("dve", "scalar_tensor_tensor", [pk, "cw", ack], [ack], out=ac[:, 0:Tn - 1], in0=pb[:, 1:Tn], scalar=cw[:, c, 2:3], in1=ac[:, 0:Tn - 1],
              op0=ALU.mult, op1=ALU.add)
            if halo:
                O("dve", "scalar_tensor_tensor", [hbk, "cw", ack], [ack], out=ac[:, 0:1], in0=hb[:, 0:1], scalar=cw[:, c, 0:1], in1=ac[:, 0:1],
                  op0=ALU.mult, op1=ALU.add)
                O("dve", "scalar_tensor_tensor", [hbk, "cw", ack], [ack], out=ac[:, Tn - 1:Tn], in0=hb[:, 1:2], scalar=cw[:, c, 2:3], in1=ac[:, Tn - 1:Tn],
                  op0=ALU.mult, op1=ALU.add)
            t0, k0 = tmp()
            O("act", "activation", [ack], [k0], out=t0[:, 0:Tn], in_=ac[:, 0:Tn], func=AF.Square, scale=GC0)
            O("dve", "scalar_tensor_tensor", [k0, ack], [(k0, "b")], out=t0[:, Tn:2 * Tn], in0=t0[:, 0:Tn], scalar=1.0, in1=ac[:, 0:Tn],
              op0=ALU.add, op1=ALU.mult)
            O("act", "activation", [(k0, "b")], [k0], out=t0[:, 0:Tn], in_=t0[:, Tn:2 * Tn], func=AF.Tanh, scale=GC1)
            O("dve", "scalar_tensor_tensor", [ack, pk], [(ack, "b")], out=ac[:, Tn:2 * Tn], in0=ac[:, 0:Tn], scalar=0.5, in1=pb[:, Tn:2 * Tn],
              op0=ALU.mult, op1=ALU.mult)
            zi = nxt("zT", 4)
            O("dve", "scalar_tensor_tensor", [k0, (ack, "b")], [("zT", zi)], out=zT[zi][:, 0:Tn], in0=t0[:, 0:Tn], scalar=1.0, in1=ac[:, Tn:2 * Tn],
              op0=ALU.add, op1=ALU.mult)
            if c % 4 == 0:
                st_["wdn"] = wload(("dn", c // 4))
            wv, wkk = st_["wdn"]
            for s in range(nt):
                for n in range(2):
                    bi = 3 + 2 * s + n
                    O("pe", "matmul", [wkk, ("zT", zi)], ["B%d" % bi], out=PB[bi][:, :], lhsT=zT[zi][:, s * 128:(s + 1) * 128],
                      rhs=wv[:, c % 4, n * 512:(n + 1) * 512], start=(c == 0), stop=(c == NFC - 1))
        Gk = [("G", ci * 2 + 1, 0), ("G", ci * 2 + 1, 1)]
        for s in range(nt):
            yi = nxt("yout", 2)
            post_norm_residual(s, xslot, xi, G[ci * 2 + 1], Gk, yout[yi], [("yout", yi, 0), ("yout", yi, 1)], 56)
            o = dma("pool", t["y"][s * 128:(s + 1) * 128, :], yout[yi][:], [("yout", yi, 0), ("yout", yi, 1)], [], ("yo", yi))
            out_dmas.append(o)

    def cache_to_scratch():
        for kt in range(4):
            xi = kt % 2
            dma("sp", xsl[xi][:, 0, :], ck[kt * 128:(kt + 1) * 128, :], [], [("x", xi, 0)], ("x", xi, 0))
            dma("sp", xsl[xi][:, 1, :], cv[kt * 128:(kt + 1) * 128, :], [], [("x", xi, 1)], ("x", xi, 1))
            xi2 = nxt("xn", 2)
            O("dve", "tensor_copy", [("x", xi, 0)], [("xn", xi2)], out=xn[xi2][:], in_=xsl[xi][:, 0, :])
            for h in range(8):
                O("pe", "transpose", [("xn", xi2), "ident"], ["B0"], out=B0[:, h, :], in_=xn[xi2][:, h * 128:(h + 1) * 128], identity=ident[:])
            O("dve", "tensor_copy", ["B0"], KLOC, out=kloc[:, :, (kt % 2) * 128:(kt % 2 + 1) * 128], in_=B0[:, :, :])
            O("act", "activation", [("x", xi, 1)], [("vloc", kt % 2)], out=vloc[:, kt % 2, :, 0:128], in_=xsl[xi][:, 1, :].rearrange("p (h e) -> p h e", h=8),
              func=AF.Copy)
            if kt % 2 == 1:
                k0 = (kt - 1) * 128
                dma("pool", Kscr[:, :, k0:k0 + 256].rearrange("h p k -> p h k"), kloc[:, :, 0:256], KLOC, [("Kscr", h) for h in range(8)], ("ksc", 0))
                for s in range(2):
                    dma("pool", Vscr[:, :, kt - 1 + s, :].rearrange("h p e -> p h e"), vloc[:, s, :, :], [("vloc", s)], [("Vscr", h) for h in range(8)], ("vsc", s))

    s1w = {}

    def s1_weights():
        names = [("in", 4), ("in", 5), ("in", 6), ("in", 7)]
        for name in names:
            if name not in converted:
                convert_into(name, nxt("ring", NRING))
        for i, name in enumerate(names):
            gi, nkc, ncols = groups[name]
            dma("sp", ring[i][:, :], Wscr[gi, :, :], [("Wscr", gi)], [("w", i)], ("w", i))
            s1w[name] = (ring[i][:, :].rearrange("p (k n) -> p k n", n=512), [("w", i)])

    s1ids = {}

    def s1_pre(j):
        xi = j % 2
        xslot = xsl[xi]
        for s in range(NT):
            dma("sp", xslot[:, s, :], xs[j * T + s * 128:j * T + (s + 1) * 128, :], [], [("x", xi, s)], ("x", xi, s))
        ri = nxt("rope", 2)
        dma("sp", rCt[ri][:, :, :], ropeCt[j * T:(j + 1) * T, :].rearrange("(s p) c -> p s c", p=128), [], [("rC", ri)], ("rC", ri))
        dma("sp", rSt[ri][:, :, :], ropeSt[j * T:(j + 1) * T, :].rearrange("(s p) c -> p s c", p=128), [], [("rS", ri)], ("rS", ri))
        s1ids[j] = (norm_stats(xslot, xi, NT, 0), ri)

    def s1_tile(j, nxt_pre):
        pmset[0] = [1, 2, 7, 3, 4, 5, 6]
        ids, ri = s1ids[j]
        norm_trans(ids, scA[:, :, 1], modF[:, 0:8, 1], hT, "hT")
        if nxt_pre is not None:
            s1_pre(nxt_pre)
        pmset[0] = [1, 2, 3, 4, 5, 6]
        kpend = []

        def k_flush():
            g_, s_, kb_, tqk_ = kpend.pop(0)
            tb_, tbk_ = TB[g_]
            for hh in range(4):
                O("pe", "transpose", [tqk_, "ident"], [tbk_], out=tb_[:, s_ * 4 + hh, :], in_=kb_[:, hh * 128:(hh + 1) * 128], identity=ident[:])
            if s_ % 2:
                O("act", "activation", [tbk_], [("kloc", g_ * 4 + hh) for hh in range(4)], out=kloc[:, g_ * 4:(g_ + 1) * 4, s_ * 128:(s_ + 1) * 128],
                  in_=tb_[:, s_ * 4:(s_ + 1) * 4, :], func=AF.Copy)
            else:
                O("dve", "tensor_copy", [tbk_], [("kloc", g_ * 4 + hh) for hh in range(4)], out=kloc[:, g_ * 4:(g_ + 1) * 4, s_ * 128:(s_ + 1) * 128],
                  in_=tb_[:, s_ * 4:(s_ + 1) * 4, :])

        for g in range(2):
            wv, wkl = s1w[("in", 4 + g)]
            for s in range(NT):
                pb, pk = pmbank()
                for kc in range(8):
                    O("pe", "matmul", wkl + [("hT", s)], [pk], out=pb[:, :], lhsT=hT[:, kc, s * 128:(s + 1) * 128], rhs=wv[:, kc, :], start=(kc == 0), stop=(kc == 7))
                tq, tqk = tmp()
                kb = tq[:, 0:256].bitcast(BF16)
                rope_tok(pb[:, :], pk, ri, s, kb, tqk)
                kpend.append((g, s, kb, tqk))
                if len(kpend) > 1:
                    k_flush()
        k0 = 512 + j * T
        for g in range(2):
            wv, wkl = s1w[("in", 6 + g)]
            if g == 1:
                while kpend:
                    k_flush()
                dma("pool", Kscr[:, :, k0:k0 + T].rearrange("h p k -> p h k"), kloc[:, :, :], KLOC, [("Kscr", h) for h in range(8)], ("ksc", 0))
            for s in range(NT):
                pb, pk = pmbank()
                for kc in range(8):
                    O("pe", "matmul", wkl + [("hT", s)], [pk], out=pb[:, :], lhsT=hT[:, kc, s * 128:(s + 1) * 128], rhs=wv[:, kc, :], start=(kc == 0), stop=(kc == 7))
                O("act", "activation", [pk], [("vloc", s)], out=vloc[:, s, g * 4:(g + 1) * 4, 0:128], in_=pb[:, :].rearrange("p (h e) -> p h e", h=4), func=AF.Copy)
        for s in range(NT):
            dma("pool", Vscr[:, :, 4 + j * NT + s, :].rearrange("h p e -> p h e"), vloc[:, s, :, :], [("vloc", s)], [("Vscr", h) for h in range(8)], ("vsc", s))

    phase0()
    ptiles = [dict(id=p, kind="p", T=T, ci=0, xi=p % 3, hi=p % 2, src=xp[p * T:(p + 1) * T, :], y=yp[p * T:(p + 1) * T, :],
                   nk=nk[p * T:(p + 1) * T, :], nv=nv[p * T:(p + 1) * T, :]) for p in range(NP_TILES)]
    if NP_TILES:
        mixer_pre(ptiles[0])
    for p in range(NP_TILES):
        mixer(ptiles[p])
        S.mark('mixer_done')
        if p + 1 < NP_TILES:
            mixer_pre(ptiles[p + 1])
        ffn(ptiles[p], None, None)
        S.mark('ffn_done')
    if DO_SAMPLE:
        cache_to_scratch()
        s1_weights()
        NS1 = 4096 // T
        s1_pre(0)
        for j in range(NS1):
            s1_tile(j, j + 1 if j + 1 < NS1 else None)
        S.mark('s1_done')
        NB = 15
        tn = dict(id=NB, kind="nb", T=128, ci=1, xi=2, hi=0, src=xs[2048:2176, :], roff=2048)
        tiles = []
        for i in range(NS_TILES):
            tiles.append(dict(id=i, kind="s", T=T, ci=1, xi=i % 3, hi=(i + 1) % 2, src=xs[i * T:(i + 1) * T, :], roff=i * T,
                              y=ys[i * T:(i + 1) * T, :]))

        def halo(i):
            if i == 0:
                hl = (HL[:, :, NB:NB + 1], ["HL"], 0)
            else:
                hl = (HL[:, :, i - 1:i], ["HL"], None)
            if i == NS_TILES - 1:
                hr = (HR[:, :, NB:NB + 1], ["HR"], 1)
            else:
                hr = (HR[:, :, i + 1:i + 2], ["HR"], None)
            return hl, hr

        mixer_pre(tn)
        mixer(tn)
        mixer_pre(tiles[0])
        for i in range(NS_TILES):
            mixer(tiles[i])
            if i + 1 < NS_TILES:
                mixer_pre(tiles[i + 1])
            if i >= 1:
                ffn(tiles[i - 1], *halo(i - 1))
        ffn(tiles[NS_TILES - 1], *halo(NS_TILES - 1))

    S.limit = 1000000000
    fin = S.op("sp", lambda e: None, [], [])
    fin.deps = [o for o in out_dmas if o.dma is not None]
    if os.environ.get("MK_MARK"):
        print("SBUF free bytes/partition", nc.sbuf_bytes_remaining)
    with ExitStack() as stack:
        S.emit(stack)
    return nc


def _rope_tables():
    n_freq = 16
    inv = (10000.0 ** (-np.arange(n_freq, dtype=np.float32) / n_freq)).astype(np.float32)
    tok = np.arange(4096)
    row = (tok // 64).astype(np.float32)
    col = (tok % 64).astype(np.float32)
    ang = np.concatenate([row[:, None] * inv, col[:, None] * inv], axis=-1).astype(np.float32)
    cos = np.cos(ang).astype(np.float32)
    sin = np.sin(ang).astype(np.float32)
    d = np.arange(64)
    C = cos[:, d % 32]
    Sg = sin[:, d % 32] * np.where(d < 32, -1.0, 1.0)[None, :]
    return np.ascontiguousarray(C, np.float32), np.ascontiguousarray(Sg, np.float32)


def _fm(v, nchunk):
    return np.ascontiguousarray(np.asarray(v, np.float32).reshape(nchunk, 128).T)


_NC_CACHE = {}


def kernel(x_prompt, x_sample, c, cache_k, cache_v, c_ctx, w_ada, b_ada, g_pre_mix, g_post_mix,
           g_pre_ffn, g_post_ffn, w_in, g_sgu, w_s, b_s, lam_q1, lam_k1, lam_q2, lam_k2, g_subln,
           w_a, w_b, w_o, w_up, conv_w, conv_b, w_down):
    f = lambda a: np.ascontiguousarray(np.asarray(a, dtype=np.float32))
    x_prompt, x_sample, c, cache_k, cache_v, c_ctx = map(f, (x_prompt, x_sample, c, cache_k, cache_v, c_ctx))
    w_in0 = f(w_in)[0]
    C, Sg = _rope_tables()
    shared = {
        "w_ada": f(w_ada)[0], "b_adaF": _fm(f(b_ada)[0], 48), "b_ada": f(b_ada)[0],
        "gpreF": np.ascontiguousarray(np.stack([_fm(f(g_pre_mix)[0], 8), _fm(f(g_pre_ffn)[0], 8)], axis=-1)),
        "gpost": np.ascontiguousarray(np.stack([f(g_post_mix)[0], f(g_post_ffn)[0]], axis=0)),
        "w_in": w_in0,
        "gsguF": _fm(f(g_sgu)[0], 4),
        "w_sT": np.ascontiguousarray(f(w_s)[0].transpose(2, 0, 1)),
        "b_s": np.ascontiguousarray(f(b_s)[0].reshape(512)),
        "lamv": np.ascontiguousarray(np.stack([f(lam_q1)[0], f(lam_k1)[0], f(lam_q2)[0], f(lam_k2)[0]], axis=0)),
        "gsub": f(g_subln)[0],
        "w_a": f(w_a)[0], "w_b": f(w_b)[0], "w_o": f(w_o)[0], "w_up": f(w_up)[0],
        "convF": np.ascontiguousarray(np.concatenate([f(conv_w)[0], f(conv_b)], axis=0).T.reshape(NFC, 128, 4).transpose(1, 0, 2)),
        "w_down": f(w_down)[0],
        "ident": np.eye(128).astype(ml_dtypes.bfloat16),
    }
    in_maps = []
    for core in range(8):
        b, half = core // 2, core % 2
        if half == 0:
            order = np.concatenate([np.arange(0, 2048), np.arange(2048, 2176), np.arange(2176, 4096)])
            flags = np.tile(np.array([[0.0, 1.0]], np.float32), (128, 1))
        else:
            order = np.concatenate([np.arange(2048, 4096), np.arange(1920, 2048), np.arange(0, 1920)])
            flags = np.tile(np.array([[1.0, 0.0]], np.float32), (128, 1))
        m = dict(shared)
        m["xp"] = np.ascontiguousarray(x_prompt[4 * core:4 * core + 4].reshape(4 * T, D))
        m["xs"] = np.ascontiguousarray(x_sample[b][order])
        m["cvec"] = np.ascontiguousarray(np.stack([_fm(c_ctx, 8), _fm(c[b], 8)], axis=-1))
        m["ck"] = np.ascontiguousarray(cache_k[b, 0].reshape(512, D))
        m["cv"] = np.ascontiguousarray(cache_v[b, 0].reshape(512, D))
        m["ropeCt"] = np.ascontiguousarray(C[order])
        m["ropeSt"] = np.ascontiguousarray(Sg[order])
        m["flags"] = flags
        in_maps.append(m)
    if "nc" not in _NC_CACHE:
        _NC_CACHE["nc"] = build_program()
    nc = _NC_CACHE["nc"]
    res = run_bass_kernel_spmd(nc, in_maps, core_ids=list(range(8)))
    r = res.results
    y_prompt = np.concatenate([r[i]["yp"].reshape(4, T, D) for i in range(8)], axis=0).astype(np.float32)
    y_sample = np.stack([np.concatenate([r[2 * b]["ys"], r[2 * b + 1]["ys"]], axis=0) for b in range(4)], axis=0).astype(np.float32)
    new_k = np.concatenate([r[i]["nk"].reshape(4, 1, T, NH, 128) for i in range(8)], axis=0).astype(np.float32)
    new_v = np.concatenate([r[i]["nv"].reshape(4, 1, T, NH, 128) for i in range(8)], axis=0).astype(np.float32)
    return (y_prompt, y_sample, new_k, new_v)
```

```python
import os
import math
from contextlib import ExitStack
import numpy as np
import ml_dtypes
import concourse.bass as bass
import concourse.mybir as mybir
from concourse.bass_utils import run_bass_kernel_spmd

F32 = mybir.dt.float32
BF16 = mybir.dt.bfloat16
AF = mybir.ActivationFunctionType
ALU = mybir.AluOpType
AX = mybir.AxisListType

D = 1024
T = 256
NT = T // 128
NH = 8
DFF = 2816
NFC = DFF // 128
EPS = 1e-6
NKEY = 4608
NKT = NKEY // 128
LAM_INIT = 0.8 - 0.6 * math.exp(0.0)
GC1 = 0.7978845608028654
GC0 = math.sqrt(0.044715)
ENGS = ["pe", "act", "dve", "pool", "sp"]

NP_TILES = int(os.environ.get("MK_NP", "4"))
NS_TILES = int(os.environ.get("MK_NS", "8"))
DO_SAMPLE = os.environ.get("MK_SAMPLE", "1") == "1"


class _Op:
    __slots__ = ("eng", "fn", "deps", "need_inc", "inc_val", "dma", "sem")


class Sched:
    def __init__(self, nc):
        self.nc = nc
        self.ops = {e: [] for e in ENGS}
        self.last_w = {}
        self.readers = {}
        self.dma_keys = {}
        self.grouped = set()
        self.gi = 0
        self.limit = int(os.environ.get("MK_LIMIT", "1000000000"))

    def mark(self, name):
        if os.environ.get("MK_MARK"):
            print("MARK", name, self.gi, flush=True)

    def op(self, eng, fn, reads=(), writes=(), dma=None):
        self.gi += 1
        o = _Op()
        if self.gi > self.limit:
            o.eng, o.fn, o.dma, o.need_inc, o.inc_val, o.sem, o.deps = eng, fn, None, False, 0, None, []
            return o
        o.eng, o.fn, o.dma, o.need_inc, o.inc_val, o.sem = eng, fn, dma, dma is not None, 0, None
        deps = {}
        for k in reads:
            w = self.last_w.get(k)
            if w is not None:
                deps[id(w)] = w
            kn = k if isinstance(k, str) else k[0]
            if isinstance(kn, str) and len(kn) == 2 and kn[0] == "B" and kn[1].isdigit():
                for r in self.readers.get(k, ()):
                    if r.eng != eng:
                        deps[id(r)] = r
        for k in writes:
            w = self.last_w.get(k)
            if w is not None:
                deps[id(w)] = w
            for r in self.readers.get(k, ()):
                deps[id(r)] = r
        dl = []
        for d in deps.values():
            if d.eng == "pe" and eng == "pe" and d.dma is None and dma is None:
                continue
            d.need_inc = True
            dl.append(d)
        o.deps = dl
        for k in writes:
            self.last_w[k] = o
            self.readers[k] = []
        for k in reads:
            self.readers.setdefault(k, []).append(o)
        self.ops[eng].append(o)
        if dma is not None:
            self.dma_keys.setdefault(dma, 0)
        return o

    def emit(self, stack):
        nc = self.nc
        esem = {e: stack.enter_context(nc.semaphore("s_" + e)) for e in ENGS}
        dsem = {k: stack.enter_context(nc.semaphore("d_%d" % i)) for i, k in enumerate(self.dma_keys)}
        dcount = {k: 0 for k in self.dma_keys}
        for e in ENGS:
            c = 0
            for o in self.ops[e]:
                if o.dma is not None:
                    dcount[o.dma] += 16
                    o.inc_val = dcount[o.dma]
                    o.sem = dsem[o.dma]
                elif o.need_inc:
                    c += 1
                    o.inc_val = c
                    o.sem = esem[e]
        for e in ENGS:
            for o in self.ops[e]:
                if o.dma in self.grouped:
                    o.inc_val = dcount[o.dma]

        def run(e, engine):
            known = {}
            for o in self.ops[e]:
                need = {}
                for d in o.deps:
                    sid = id(d.sem)
                    if known.get(sid, 0) >= d.inc_val:
                        continue
                    if sid not in need or need[sid][1] < d.inc_val:
                        need[sid] = (d.sem, d.inc_val)
                for sid, (sem, val) in need.items():
                    engine.wait_ge(sem, val)
                    known[sid] = val
                ins = o.fn(engine)
                if o.need_inc:
                    ins.then_inc(o.sem, 16 if o.dma is not None else 1)

        with nc.Block() as block:
            @block.tensor
            def _(eng):
                run("pe", eng)

            @block.scalar
            def _(eng):
                run("act", eng)

            @block.vector
            def _(eng):
                run("dve", eng)

            @block.gpsimd
            def _(eng):
                run("pool", eng)

            @block.sync
            def _(eng):
                run("sp", eng)


def build_program():
    nc = bass.Bass("TRN2", target_bir_lowering=False)
    S = Sched(nc)

    def din(name, shape, dt=F32):
        return nc.dram_tensor(name, list(shape), dt, kind="ExternalInput").ap()

    def dout(name, shape):
        return nc.dram_tensor(name, list(shape), F32, kind="ExternalOutput").ap()

    xp = din("xp", [4 * T, D])
    xs = din("xs", [4096, D])
    cvec = din("cvec", [128, 8, 2])
    ck = din("ck", [512, D])
    cv = din("cv", [512, D])
    w_ada = din("w_ada", [D, 6 * D])
    b_adaF = din("b_adaF", [128, 48])
    b_ada = din("b_ada", [6 * D])
    gpreF = din("gpreF", [128, 8, 2])
    gpost = din("gpost", [2, D])
    w_in = din("w_in", [D, 6 * D])
    gsguF = din("gsguF", [128, 4])
    w_sT = din("w_sT", [128, 4, 128])
    b_s = din("b_s", [512])
    lamv = din("lamv", [4, 64])
    gsub = din("gsub", [128])
    w_a = din("w_a", [512, D])
    w_b = din("w_b", [D, D])
    w_o = din("w_o", [D, D])
    w_up = din("w_up", [D, 2 * DFF])
    convF = din("convF", [128, NFC, 4])
    w_down = din("w_down", [DFF, D])
    ropeCt = din("ropeCt", [4096, 64])
    ropeSt = din("ropeSt", [4096, 64])
    flags = din("flags", [128, 2])
    ident_d = din("ident", [128, 128], BF16)

    yp = dout("yp", [4 * T, D])
    ys = dout("ys", [2048, D])
    nk = dout("nk", [4 * T, D])
    nv = dout("nv", [4 * T, D])

    groups = {}
    glist = []

    def add_group(key, srcs, nkc, ncols):
        groups[key] = (len(glist), nkc, ncols)
        glist.append((key, srcs, nkc, ncols))

    for g in range(12):
        add_group(("in", g), [(w_in[:, g * 512:(g + 1) * 512], 0)], 8, 512)
    for g in range(2):
        add_group(("wa", g), [(w_a[:, g * 512:(g + 1) * 512], 0)], 4, 512)
    for g in range(2):
        add_group(("wb", g), [(w_b[:, g * 512:(g + 1) * 512], 0)], 8, 512)
    for g in range(2):
        add_group(("wo", g), [(w_o[:, g * 512:(g + 1) * 512], 0)], 8, 512)
    for i in range(11):
        add_group(("upp", i), [(w_up[:, i * 256:(i + 1) * 256], 0), (w_up[:, DFF + i * 256:DFF + (i + 1) * 256], 256)], 8, 512)
    for q in range(6):
        nch = 4 if q < 5 else 2
        add_group(("dn", q), [(w_down[q * 512:q * 512 + nch * 128, :], 0)], nch, 1024)
    NG = len(glist)
    Wscr = nc.dram_tensor("Wscr", [NG, 128, 4096], BF16, kind="Internal").ap()
    Kscr = nc.dram_tensor("Kscr", [NH, 128, NKEY], BF16, kind="Internal").ap()
    Vscr = nc.dram_tensor("Vscr", [NH, 128, NKT, 130], BF16, kind="Internal").ap()

    A = nc.alloc_sbuf_tensor
    NRING = 4
    TE = T + 2
    ring = [A("ring%d" % i, [128, 4096], BF16) for i in range(NRING)]
    xsl = [A("xsl%d" % i, [128, NT, D], F32) for i in range(3)]
    KVW = NKEY + NKT * 130
    KVbuf = A("KVbuf", [128, 2 * KVW], BF16)
    KTh = [KVbuf[:, i * KVW:i * KVW + NKEY] for i in range(2)]
    Vh = [KVbuf[:, i * KVW + NKEY:(i + 1) * KVW].rearrange("p (k e) -> p k e", e=130) for i in range(2)]
    kloc = A("kloc", [128, NH, T], BF16)
    vloc = A("vloc", [128, NT, NH, 130], BF16)
    hT = A("hT", [128, 8, T], BF16)
    h2T = [A("h2T%d" % i, [128, 8, TE], BF16) for i in range(2)]
    xn = [A("xn%d" % i, [128, D], BF16) for i in range(2)]
    uT = A("uT", [128, 4, T], BF16)
    vg = A("vg", [128, NT, 512], BF16)
    qm = A("qm", [128, NH, 2, T], BF16)
    aT = A("aT", [128, 4, T], BF16)
    bT = A("bT", [128, 8, T], BF16)
    mT = A("mT", [128, 8, T], BF16)
    tga = A("tga", [128, 4, T], F32)
    tgb = A("tgb", [128, 4, T], F32)
    pT = [A("pT%d" % i, [128, 2, T], BF16) for i in range(4)]
    rCt = [A("rCt%d" % i, [128, NT, 64], F32) for i in range(2)]
    rSt = [A("rSt%d" % i, [128, NT, 64], F32) for i in range(2)]
    NTMP = 9
    tmps = [A("tmp%d" % i, [128, 512], F32) for i in range(NTMP)]
    junks = [A("junk%d" % i, [128, D], BF16) for i in range(2)]
    bo = [A("bo%d" % i, [128, NT, 128], BF16) for i in range(2)]
    G = [A("G%d" % i, [128, D], F32) for i in range(4)]
    bsb = A("bsb", [128, 4, 128], F32)
    zT = [A("zT%d" % i, [128, T], BF16) for i in range(4)]
    yout = [A("yout%d" % i, [128, D], F32) for i in range(2)]
    ident = A("identsb", [128, 128], BF16)
    ones_bf = A("ones_bf", [128, 128], BF16)
    epst = A("epst", [128, 4], F32)
    cv_sb = A("cv_sb", [128, 8, 2], F32)
    csl = A("csl", [128, 8, 2], BF16)
    cslf = A("cslf", [128, 8, 2], F32)
    crep = xsl[1][:, 1, :].bitcast(BF16).rearrange("p (k c m) -> p k c m", k=8, c=2)
    CREPK = ("x", 1, 1)
    modF = A("modF", [128, 48, 2], F32)
    badF = A("badF", [128, 48], F32)
    gpre = A("gpre", [128, 8, 2], F32)
    scA = A("scA", [128, 8, 2], F32)
    scB = A("scB", [128, 8, 2], F32)
    gsg = A("gsg", [128, 4], F32)
    wsT = A("wsT", [128, 4, 128], BF16)
    wsTf = tmps[7][:, :].rearrange("p (g q) -> p g q", g=4)
    lamt = tmps[6][:, 0:256].rearrange("p (a b) -> p a b", a=4)
    lams = A("lams", [128, 8], F32)
    gsb = A("gsb", [128, 128], F32)
    cw = A("cw", [128, NFC, 4], F32)
    flg = A("flg", [128, 2], F32)
    HL = A("HL", [128, 8, 16], BF16)
    HR = A("HR", [128, 8, 16], BF16)
    st = A("st", [128, 96], F32)

    B0f = nc.alloc_psum_tensor("B0", [128, 512], F32)
    B0 = B0f[:, :].bitcast(BF16).rearrange("p (a b) -> p a b", a=8)
    PB = [B0f] + [nc.alloc_psum_tensor("B%d" % i, [128, 512], F32) for i in range(1, 8)]
    B7b = PB[7][:, :].bitcast(BF16).rearrange("p (a b) -> p a b", a=8)
    TB = [(B0, "B0"), (B7b, "B7")]

    KLOC = [("kloc", h) for h in range(8)]

    def O(eng, meth, reads, writes, **kw):
        return S.op(eng, lambda e: getattr(e, meth)(**kw), reads, writes)

    def dma(eng, out, in_, reads, writes, key):
        return S.op(eng, lambda e: e.dma_start(out=out, in_=in_), reads, writes, dma=key)

    out_dmas = []
    cnt = {"ring": 0, "xn": 0, "pT": 0, "pm": 0, "ps": 0, "zT": 0, "yout": 0, "kv": 0, "rope": 0, "stg": 0, "tmp": 0, "jk": 0, "fa": 0}
    pmset = [[1, 2, 7, 3, 4, 5, 6]]

    def nxt(name, n):
        v = cnt[name] % n
        cnt[name] += 1
        return v

    def tmp():
        i = nxt("tmp", NTMP)
        return tmps[i], ("tmp", i)

    def jk():
        i = nxt("jk", 2)
        return junks[i], ("jk", i)

    converted = set()
    stgv = [KVbuf[:, i * KVW:i * KVW + 8192].bitcast(F32) for i in range(2)]
    cvt = {"i": 0}

    def convert_into(key, r):
        gi, nkc, ncols = groups[key]
        _, srcs, _, _ = glist[gi]
        si = nxt("stg", 2)
        n = nkc * ncols
        sv = stgv[si][:, 0:n].rearrange("p (k n) -> p k n", n=ncols)
        for (src, off) in srcs:
            wcol = src.shape[1]
            dma("sp", sv[:, :, off:off + wcol], src.rearrange("(k p) n -> p k n", p=128), [], [("KTh", si), ("Vh", si)], ("KTh", si, off))
        eng = ["act", "dve"][cvt["i"] % 2]
        cvt["i"] += 1
        if eng == "act":
            O("act", "activation", [("KTh", si), ("Vh", si)], [("w", r)], out=ring[r][:, 0:n], in_=stgv[si][:, 0:n], func=AF.Copy)
        else:
            O(eng, "tensor_copy", [("KTh", si), ("Vh", si)], [("w", r)], out=ring[r][:, 0:n], in_=stgv[si][:, 0:n])
        dma("pool", Wscr[gi, :, 0:n], ring[r][:, 0:n], [("w", r)], [("Wscr", gi)], ("wst", r))
        converted.add(key)

    def wload(key):
        gi, nkc, ncols = groups[key]
        r = nxt("ring", NRING)
        view = ring[r][:, 0:nkc * ncols].rearrange("p (k n) -> p k n", n=ncols)
        if key not in converted:
            convert_into(key, r)
        else:
            dma("sp", ring[r][:, 0:nkc * ncols], Wscr[gi, :, 0:nkc * ncols], [("Wscr", gi)], [("w", r)], ("w", r))
        return view, ("w", r)

    def pmbank():
        ps_ = pmset[0]
        i = ps_[nxt("pm", len(ps_))]
        return PB[i], "B%d" % i

    def phase0():
        cst = "cst"
        pmset[0] = [1, 2]
        dma("sp", ident[:], ident_d, [], ["ident"], cst)
        dma("sp", cv_sb[:], cvec, [], ["cv_sb"], cst)
        dma("sp", badF[:], b_adaF, [], ["badF"], cst)
        dma("sp", gpre[:], gpreF, [], ["gpre"], cst)
        dma("sp", gsg[:], gsguF, [], ["gsg"], cst)
        dma("sp", wsTf, w_sT, [], [("tmp", 7)], cst)
        dma("sp", bsb[:].rearrange("p g q -> p (g q)"), b_s.partition_broadcast(128), [], ["bsb"], cst)
        dma("sp", tmps[6][:, 0:256], lamv.rearrange("a b -> (a b)").partition_broadcast(128), [], [("tmp", 6)], cst)
        dma("sp", gsb[:], gsub.partition_broadcast(128), [], ["gsb"], cst)
        dma("sp", cw[:], convF, [], ["cw"], cst)
        dma("sp", flg[:], flags, [], ["flg"], cst)
        S.grouped.add(cst)
        O("pool", "memset", [], ["ones_bf"], ap=ones_bf[:], constant=1.0)
        O("pool", "memset", [], ["epst"], ap=epst[:, 0:1], constant=EPS)
        O("pool", "memset", [], ["epst"], ap=epst[:, 1:2], constant=4 * EPS)
        O("pool", "memset", [], [("qm", h_) for h_ in range(NH)], ap=qm[:], constant=0.0)
        O("pool", "memset", [], [("vloc", s_) for s_ in range(NT)], ap=vloc[:], constant=1.0)
        O("pool", "memset", [], ["HL", "HR"], ap=HL[:], constant=0.0)
        O("pool", "memset", [], ["HL", "HR"], ap=HR[:], constant=0.0)
        S.mark('p0_consts')
        O("act", "activation", ["cv_sb"], ["cslf"], out=cslf[:], in_=cv_sb[:], func=AF.Silu)
        O("dve", "tensor_copy", ["cslf"], ["csl"], out=csl[:], in_=cslf[:])
        for kc in range(8):
            for ci in range(2):
                O("dve", "tensor_scalar", ["cslf", "ones_bf"], [CREPK], out=crep[:, kc, ci, :], in0=ones_bf[:],
                  scalar1=cslf[:, kc, ci:ci + 1], scalar2=None, op0=ALU.mult)
        O("dve", "tensor_copy", [("tmp", 7)], ["wsT"], out=wsT[:], in_=wsTf)
        O("dve", "tensor_scalar", ["gsb"], ["gsb"], out=gsb[:], in0=gsb[:], scalar1=(1.0 - LAM_INIT) * 0.5, scalar2=None, op0=ALU.mult)
        O("dve", "tensor_tensor", [("tmp", 6)], [("tmp", 6)], out=lamt[:, 0, :], in0=lamt[:, 0, :], in1=lamt[:, 1, :], op=ALU.mult)
        O("dve", "tensor_tensor", [("tmp", 6)], [("tmp", 6)], out=lamt[:, 2, :], in0=lamt[:, 2, :], in1=lamt[:, 3, :], op=ALU.mult)
        O("dve", "reduce_sum", [("tmp", 6)], ["lams"], out=lams[:, 0:1], in_=lamt[:, 0, :], axis=AX.X)
        O("dve", "reduce_sum", [("tmp", 6)], ["lams"], out=lams[:, 1:2], in_=lamt[:, 2, :], axis=AX.X)
        O("act", "activation", ["lams"], ["lams"], out=lams[:, 2:4], in_=lams[:, 0:2], func=AF.Exp)
        O("dve", "tensor_tensor", ["lams"], ["lams"], out=lams[:, 4:5], in0=lams[:, 3:4], in1=lams[:, 2:3], op=ALU.subtract)
        O("dve", "tensor_scalar", ["lams"], ["neglam"], out=lams[:, 5:6], in0=lams[:, 4:5], scalar1=-LAM_INIT, scalar2=None, op0=ALU.add)

        S.mark('p0_lam')
        stg = stgv
        S.mark('p0_prepass')
        adag = {}
        for g in range(12):
            si = nxt("stg", 2)
            dma("sp", stg[si][:].rearrange("p (k n) -> p k n", n=512), w_ada[:, g * 512:(g + 1) * 512].rearrange("(k p) n -> p k n", p=128),
                [], [("KTh", si), ("Vh", si)], ("KTh", si, 0))
            r = nxt("ring", NRING)
            if g % 2:
                O("dve", "tensor_copy", [("KTh", si), ("Vh", si)], [("w", r)], out=ring[r][:], in_=stg[si][:])
            else:
                O("act", "activation", [("KTh", si), ("Vh", si)], [("w", r)], out=ring[r][:], in_=stg[si][:], func=AF.Copy)
            wv = ring[r][:].rearrange("p (k n) -> p k n", n=512)
            pmod = PB[7]
            for blk in range(4):
                j = g * 4 + blk
                for kc in range(8):
                    O("pe", "matmul", [("w", r), "csl"], ["B7"], out=pmod[:, 2 * j:2 * j + 2], lhsT=wv[:, kc, blk * 128:(blk + 1) * 128],
                      rhs=csl[:, kc, :], start=(kc == 0), stop=(kc == 7))
            if g in (4, 5, 10, 11):
                which = 0 if g < 6 else 1
                half = g % 2
                if half == 0:
                    base = 2048 if which == 0 else 5120
                    dma("sp", xsl[0][:, 0, :], b_ada[base:base + 1024].partition_broadcast(128), [], [("x", 0, 0)], ("x", 0, 0))
                    dma("sp", xsl[1][:, 0, :], gpost[which].partition_broadcast(128), [], [("x", 1, 0)], ("x", 1, 0))
                for ci2 in range(2):
                    pb, pk = pmbank()
                    for kc in range(8):
                        O("pe", "matmul", [("w", r), CREPK], [pk], out=pb[:, :], lhsT=crep[:, kc, ci2, :], rhs=wv[:, kc, :],
                          start=(kc == 0), stop=(kc == 7))
                    gdst = G[ci2 * 2 + which][:, half * 512:(half + 1) * 512]
                    O("dve", "tensor_tensor", [pk, ("x", 0, 0)], [("G", ci2 * 2 + which, half)], out=gdst, in0=pb[:, :],
                      in1=xsl[0][:, 0, half * 512:(half + 1) * 512], op=ALU.add)
                    O("dve", "tensor_tensor", [("G", ci2 * 2 + which, half), ("x", 1, 0)], [("G", ci2 * 2 + which, half)], out=gdst, in0=gdst,
                      in1=xsl[1][:, 0, half * 512:(half + 1) * 512], op=ALU.mult)
        S.mark('p0_ada')
        O("dve", "tensor_tensor", ["B7", "badF"], ["modF"], out=modF[:, :, 0], in0=PB[7][:, 0:96].rearrange("p (j c) -> p j c", c=2)[:, :, 0],
          in1=badF[:], op=ALU.add)
        O("dve", "tensor_tensor", ["B7", "badF"], ["modF"], out=modF[:, :, 1], in0=PB[7][:, 0:96].rearrange("p (j c) -> p j c", c=2)[:, :, 1],
          in1=badF[:], op=ALU.add)
        for ci2 in range(2):
            O("dve", "scalar_tensor_tensor", ["modF", "gpre"], ["scA"], out=scA[:, :, ci2], in0=modF[:, 8:16, ci2], scalar=1.0,
              in1=gpre[:, :, 0], op0=ALU.add, op1=ALU.mult)
            O("dve", "scalar_tensor_tensor", ["modF", "gpre"], ["scB"], out=scB[:, :, ci2], in0=modF[:, 32:40, ci2], scalar=1.0,
              in1=gpre[:, :, 1], op0=ALU.add, op1=ALU.mult)

    def rstd_from(ss_ap, n, eps_col, out_ap, rk, wkey):
        O("act", "activation", rk + ["epst"], [wkey + "_ln", wkey], out=out_ap, in_=ss_ap, func=AF.Ln, scale=1.0 / n, bias=epst[:, eps_col:eps_col + 1])
        O("act", "activation", [wkey + "_ln"], [wkey], out=out_ap, in_=out_ap, func=AF.Exp, scale=-0.5)

    def norm_stats(xslot, xi, nt, sbase):
        ids = []
        for s in range(nt):
            xk = ("x", xi, s)
            j, jkk = jk()
            O("act", "activation", [xk], [jkk, ("ss", sbase, s)], out=j[:], in_=xslot[:, s, :], func=AF.Square,
              accum_out=st[:, sbase + s:sbase + s + 1])
            rstd_from(st[:, sbase + s:sbase + s + 1], D, 0, st[:, sbase + 4 + s:sbase + 5 + s], [("ss", sbase, s)], "rs%d_%d" % (sbase, s))
            xi2 = nxt("xn", 2)
            O("dve", "tensor_scalar", [xk, "rs%d_%d" % (sbase, s)], [("xn", xi2)], out=xn[xi2][:], in0=xslot[:, s, :],
              scalar1=st[:, sbase + 4 + s:sbase + 5 + s], scalar2=None, op0=ALU.mult)
            ids.append(xi2)
        return ids

    def norm_trans(ids, scale_ap, bias_ap, dst, dkey, coff=0):
        for s, xi2 in enumerate(ids):
            for kc in range(8):
                O("pe", "transpose", [("xn", xi2), "ident"], ["B0"], out=B0[:, kc, :], in_=xn[xi2][:, kc * 128:(kc + 1) * 128], identity=ident[:])
            for kc in range(8):
                O("dve", "tensor_scalar", ["B0", "scA", "scB", "modF"], [(dkey, s)], out=dst[:, kc, coff + s * 128:coff + (s + 1) * 128], in0=B0[:, kc, :],
                  scalar1=scale_ap[:, kc:kc + 1], scalar2=bias_ap[:, kc:kc + 1], op0=ALU.mult, op1=ALU.add)

    def norm_T(xslot, xi, nt, scale_ap, bias_ap, dst, dkey, sbase, coff=0):
        for s in range(nt):
            ids = norm_stats_one(xslot, xi, s, sbase)
            norm_trans_one(ids, s, scale_ap, bias_ap, dst, dkey, coff)

    def norm_stats_one(xslot, xi, s, sbase):
        xk = ("x", xi, s)
        j, jkk = jk()
        O("act", "activation", [xk], [jkk, ("ss", sbase, s)], out=j[:], in_=xslot[:, s, :], func=AF.Square,
          accum_out=st[:, sbase + s:sbase + s + 1])
        rstd_from(st[:, sbase + s:sbase + s + 1], D, 0, st[:, sbase + 4 + s:sbase + 5 + s], [("ss", sbase, s)], "rs%d_%d" % (sbase, s))
        xi2 = nxt("xn", 2)
        O("dve", "tensor_scalar", [xk, "rs%d_%d" % (sbase, s)], [("xn", xi2)], out=xn[xi2][:], in0=xslot[:, s, :],
          scalar1=st[:, sbase + 4 + s:sbase + 5 + s], scalar2=None, op0=ALU.mult)
        return xi2

    def norm_trans_one(xi2, s, scale_ap, bias_ap, dst, dkey, coff=0):
        for kc in range(8):
            O("pe", "transpose", [("xn", xi2), "ident"], ["B0"], out=B0[:, kc, :], in_=xn[xi2][:, kc * 128:(kc + 1) * 128], identity=ident[:])
        for kc in range(8):
            O("dve", "tensor_scalar", ["B0", "scA", "scB", "modF"], [(dkey, s)], out=dst[:, kc, coff + s * 128:coff + (s + 1) * 128], in0=B0[:, kc, :],
              scalar1=scale_ap[:, kc:kc + 1], scalar2=bias_ap[:, kc:kc + 1], op0=ALU.mult, op1=ALU.add)

    def gelu2(src, skey, n, dst, dkeys):
        t0, k0 = tmp()
        t1, k1 = tmp()
        O("act", "activation", [skey], [k0], out=t0[:, 0:n], in_=src, func=AF.Square, scale=GC0)
        O("dve", "scalar_tensor_tensor", [k0, skey], [k1], out=t1[:, 0:n], in0=t0[:, 0:n], scalar=1.0, in1=src, op0=ALU.add, op1=ALU.mult)
        O("act", "activation", [k1], [k0], out=t0[:, 0:n], in_=t1[:, 0:n], func=AF.Tanh, scale=GC1)
        O("dve", "scalar_tensor_tensor", [k0, skey], dkeys, out=dst, in0=t0[:, 0:n], scalar=1.0, in1=src, op0=ALU.add, op1=ALU.mult)

    def rope_tok(pb, pk, ri, s_, out_bf, okey):
        t4, k4 = tmp()
        t5, k5 = tmp()
        Ct = rCt[ri][:, s_, :]
        St = rSt[ri][:, s_, :].rearrange("p (h c) -> p h c", h=2)
        O("dve", "tensor_tensor", [pk, ("rC", ri)], [k4], out=t4[:, :].rearrange("p (a c) -> p a c", c=64),
          in0=pb.rearrange("p (a c) -> p a c", c=64), in1=Ct.unsqueeze(1).broadcast_to([128, 8, 64]), op=ALU.mult)
        pv4 = pb.rearrange("p (a h c) -> p a h c", h=2, c=32)
        t5v = t5[:, :].rearrange("p (a h c) -> p a h c", h=2, c=32)
        for half in range(2):
            O("dve", "tensor_tensor", [pk, ("rS", ri)], [(k5, half)], out=t5v[:, :, half, :], in0=pv4[:, :, 1 - half, :],
              in1=St[:, half, :].unsqueeze(1).broadcast_to([128, 8, 32]), op=ALU.mult)
        O("dve", "tensor_tensor", [k4, (k5, 0), (k5, 1)], [okey], out=out_bf, in0=t4[:, :], in1=t5[:, :], op=ALU.add)

    SB = [3, 4, 7]

    def attention_all(Tn, nt, sample, nkt):
        ctx = {}

        def head_ctx(h):
            if h not in ctx:
                if sample:
                    ki = nxt("kv", 2)
                    dma("sp", KTh[ki], Kscr[h], [("Kscr", h)], [("KTh", ki)], ("KTh", ki))
                    dma("sp", Vh[ki], Vscr[h], [("Vscr", h)], [("Vh", ki)], ("Vh", ki))
                    ctx[h] = (KTh[ki], [("KTh", ki)], (lambda kt, ki=ki: Vh[ki][:, kt, 0:129]), [("Vh", ki)])
                else:
                    ctx[h] = (kloc[:, h, :], [("kloc", h)], (lambda kt, h=h: vloc[:, kt, h, 0:129]), [("vloc", s_) for s_ in range(nt)])
            return ctx[h]

        def scores(u):
            h, kt = u
            KT_ap, kkeys, _, _ = head_ctx(h)
            si = SB[nxt("ps", 3)]
            ps, psk = PB[si], "B%d" % si
            for m in range(2):
                O("pe", "matmul", kkeys + [("qm", h)], [psk], out=ps[:, m * Tn:(m + 1) * Tn], lhsT=KT_ap[:, kt * 128:(kt + 1) * 128],
                  rhs=qm[:, h, m, 0:Tn], start=True, stop=True)
            return ps, psk

        units = [(h, kt) for h in range(NH) for kt in range(nkt)]
        head_ctx(0)
        pend = [scores(units[0]), scores(units[1])]
        stages = {2: 0, 6: 1, 9: 2, 12: 3, 15: 4, 19: 5}
        for i, (h, kt) in enumerate(units):
            if kt == 0 and h + 1 < NH:
                head_ctx(h + 1)
            _, _, V_of, vkeys = head_ctx(h)
            pbi = [5, 6] if h % 2 == 0 else [1, 2]
            ps, psk = pend.pop(0)
            if i + 2 < len(units):
                pend.append(scores(units[i + 2]))
            pi = nxt("pT", 4)
            O("act", "activation", [psk], [("pT", pi)], out=pT[pi][:, :, 0:Tn], in_=ps[:, 0:2 * Tn].rearrange("p (m t) -> p m t", m=2),
              func=AF.Exp, scale=0.125)
            for s in range(nt):
                for m in range(2):
                    O("pe", "matmul", vkeys + [("pT", pi)], ["B%d" % pbi[s]], out=PB[pbi[s]][:, m * 129:(m + 1) * 129],
                      lhsT=pT[pi][:, m, s * 128:(s + 1) * 128], rhs=V_of(kt), start=(kt == 0 and m == 0), stop=(kt == nkt - 1 and m == 1))
            if h >= 1:
                for kk, stg_ in stages.items():
                    if kt == min(kk, nkt - 1):
                        if stg_ == 5 and nkt <= 2:
                            if h >= 2:
                                attention_fin(h - 2, nt, 5)
                        else:
                            attention_fin(h - 1, nt, stg_)

    fin_state = {}

    def attention_fin(h, nt, stage):
        par = h % 2
        pbi = [5, 6] if par == 0 else [1, 2]
        if stage == 5:
            attention_fin_b(h, nt)
            return
        if stage == 0:
            fin_state[h] = (nxt("fa", 2), [tmp() for _ in range(nt)])
        fa, tws = fin_state[h]
        c0 = 16 + 16 * fa
        fk = ("fin", fa)
        if stage == 2:
            O("act", "activation", [(fk, "ss", s) for s in range(nt)] + ["epst"], [(fk, "ln")], out=st[:, c0 + 10:c0 + 10 + nt],
              in_=st[:, c0 + 8:c0 + 8 + nt], func=AF.Ln, scale=1.0 / 128, bias=epst[:, 0:1])
            return
        if stage == 3:
            O("act", "activation", [(fk, "ln")], [(fk, "rstd")], out=st[:, c0 + 12:c0 + 12 + nt], in_=st[:, c0 + 10:c0 + 10 + nt], func=AF.Exp, scale=-0.5)
            return
        for s in range(nt):
            pk = "B%d" % pbi[s]
            pov = PB[pbi[s]][:, 0:258].rearrange("p (m e) -> p m e", m=2)
            tw, twk = tws[s]
            cs = c0 + 4 * s
            if stage == 0:
                O("dve", "reciprocal", [pk], [(fk, 0, s)], out=st[:, cs:cs + 2], in_=pov[:, :, 128])
                O("dve", "tensor_tensor", [(fk, 0, s), "neglam"], [(fk, 1, s)], out=st[:, cs + 2:cs + 3], in0=st[:, cs + 1:cs + 2],
                  in1=lams[:, 5:6], op=ALU.mult)
                O("dve", "tensor_scalar", [pk, (fk, 0, s)], [twk], out=tw[:, 0:128], in0=pov[:, 0, 0:128],
                  scalar1=st[:, cs:cs + 1], scalar2=None, op0=ALU.mult)
                O("dve", "scalar_tensor_tensor", [pk, (fk, 1, s), twk], [(twk, "b")], out=tw[:, 128:256], in0=pov[:, 1, 0:128],
                  scalar=st[:, cs + 2:cs + 3], in1=tw[:, 0:128], op0=ALU.mult, op1=ALU.add)
            elif stage == 1:
                O("dve", "tensor_tensor", [(twk, "b")], [(twk, "c")], out=tw[:, 256:384], in0=tw[:, 128:256], in1=tw[:, 128:256], op=ALU.mult)
                O("dve", "reduce_sum", [(twk, "c")], [(fk, "ss", s)], out=st[:, c0 + 8 + s:c0 + 9 + s], in_=tw[:, 256:384], axis=AX.X)
            elif stage == 4:
                O("dve", "scalar_tensor_tensor", [(twk, "b"), (fk, "rstd"), "gsb"], [("bo", par, s)], out=bo[par][:, s, :],
                  in0=tw[:, 128:256], scalar=st[:, c0 + 12 + s:c0 + 13 + s], in1=gsb[:], op0=ALU.mult, op1=ALU.mult)

    def attention_fin_b(h, nt):
        par = h % 2
        for s in range(nt):
            O("pe", "transpose", [("bo", par, s), "ident"], ["B0"], out=B0[:, s, :], in_=bo[par][:, s, :], identity=ident[:])
        for s in range(nt):
            O("dve", "tensor_copy", ["B0"], [("bT", h)], out=bT[:, h, s * 128:(s + 1) * 128], in_=B0[:, s, :])

    def mixer_pre(t):
        Tn, nt, xi = t["T"], t["T"] // 128, t["xi"]
        xslot = xsl[xi]
        for s in range(nt):
            dma("sp", xslot[:, s, :], t["src"][s * 128:(s + 1) * 128, :], [], [("x", xi, s)], ("x", xi, s))
        if t["kind"] != "p":
            ri = nxt("rope", 2)
            dma("sp", rCt[ri][:, 0:nt, :], ropeCt[t["roff"]:t["roff"] + Tn, :].rearrange("(s p) c -> p s c", p=128), [], [("rC", ri)], ("rC", ri))
            dma("sp", rSt[ri][:, 0:nt, :], ropeSt[t["roff"]:t["roff"] + Tn, :].rearrange("(s p) c -> p s c", p=128), [], [("rS", ri)], ("rS", ri))
            t["ri"] = ri
        t["ids"] = norm_stats(xslot, xi, nt, 0)

    def mixer(t):
        Tn, nt, ci, xi = t["T"], t["T"] // 128, t["ci"], t["xi"]
        xslot = xsl[xi]
        sample = t["kind"] != "p"
        pmset[0] = [1, 2, 7, 3, 4, 5, 6]
        if sample:
            ri = t["ri"]
        norm_trans(t["ids"], scA[:, :, ci], modF[:, 0:8, ci], hT, "hT")
        hk = [("hT", s) for s in range(nt)]
        S.mark('m_norm1')
        wv, wkk = wload(("in", 0))
        for blk in range(4):
            pb, pk = pmbank()
            for kc in range(8):
                O("pe", "matmul", [wkk] + hk, [pk], out=pb[:, 0:Tn], lhsT=wv[:, kc, blk * 128:(blk + 1) * 128], rhs=hT[:, kc, 0:Tn],
                  start=(kc == 0), stop=(kc == 7))
            gelu2(pb[:, 0:Tn], pk, Tn, uT[:, blk, 0:Tn], [("uT", blk)])
        wv, wkk = wload(("in", 1))
        for s in range(nt):
            pb, pk = pmbank()
            for kc in range(8):
                O("pe", "matmul", [wkk, ("hT", s)], [pk], out=pb[:, :], lhsT=hT[:, kc, s * 128:(s + 1) * 128], rhs=wv[:, kc, :],
                  start=(kc == 0), stop=(kc == 7))
            tv, tvk = tmp()
            gelu2(pb[:, :], pk, 512, tv[:, :], [tvk])
            j, jkk = jk()
            O("act", "activation", [tvk], [jkk, ("ssv", s)], out=j[:, 0:512], in_=tv[:, :], func=AF.Square, accum_out=st[:, 8 + s:9 + s])
            rstd_from(st[:, 8 + s:9 + s], 512, 1, st[:, 10 + s:11 + s], [("ssv", s)], "rsv%d" % s)
            O("dve", "tensor_scalar", [tvk, "rsv%d" % s], [("vg", s)], out=vg[:, s, :], in0=tv[:, :], scalar1=st[:, 10 + s:11 + s], scalar2=None, op0=ALU.mult)
        qpend = []

        def q_flush():
            g_, s_, qb_, tqk_ = qpend.pop(0)
            tb_, tbk_ = TB[g_]
            for hh in range(4):
                O("pe", "transpose", [tqk_, "ident"], [tbk_], out=tb_[:, s_ * 4 + hh, :], in_=qb_[:, hh * 128:(hh + 1) * 128], identity=ident[:])
            O("act", "activation", [tbk_], [("qm", g_ * 4 + hh) for hh in range(4)], out=qm[0:64, g_ * 4:(g_ + 1) * 4, 0, s_ * 128:(s_ + 1) * 128],
              in_=tb_[0:64, s_ * 4:(s_ + 1) * 4, :], func=AF.Copy)
            O("dve", "tensor_copy", [tbk_], [("qm", g_ * 4 + hh) for hh in range(4)], out=qm[64:128, g_ * 4:(g_ + 1) * 4, 1, s_ * 128:(s_ + 1) * 128],
              in_=tb_[64:128, s_ * 4:(s_ + 1) * 4, :])

        for g in range(2):
            wv, wkk = wload(("in", 2 + g))
            if sample:
                pmset[0] = [1, 2, 3, 4, 5, 6]
                tb, tbk = TB[g]
                for s in range(nt):
                    pb, pk = pmbank()
                    for kc in range(8):
                        O("pe", "matmul", [wkk, ("hT", s)], [pk], out=pb[:, :], lhsT=hT[:, kc, s * 128:(s + 1) * 128], rhs=wv[:, kc, :],
                          start=(kc == 0), stop=(kc == 7))
                    tq, tqk = tmp()
                    qb = tq[:, 0:256].bitcast(BF16)
                    rope_tok(pb[:, :], pk, ri, s, qb, tqk)
                    qpend.append((g, s, qb, tqk))
                    if len(qpend) > 1:
                        q_flush()
            else:
                for blk in range(4):
                    h = g * 4 + blk
                    pb, pk = pmbank()
                    for kc in range(8):
                        O("pe", "matmul", [wkk] + hk, [pk], out=pb[:, 0:Tn], lhsT=wv[:, kc, blk * 128:(blk + 1) * 128], rhs=hT[:, kc, 0:Tn],
                          start=(kc == 0), stop=(kc == 7))
                    O("act", "activation", [pk], [("qm", h)], out=qm[0:64, h, 0, 0:Tn], in_=pb[0:64, 0:Tn], func=AF.Copy)
                    O("dve", "tensor_copy", [pk], [("qm", h)], out=qm[64:128, h, 1, 0:Tn], in_=pb[64:128, 0:Tn])
        while qpend:
            q_flush()
        pmset[0] = [1, 2, 7, 3, 4, 5, 6]
        if not sample:
            for g in range(2):
                wv, wkk = wload(("in", 4 + g))
                for blk in range(4):
                    h = g * 4 + blk
                    pb, pk = pmbank()
                    for kc in range(8):
                        O("pe", "matmul", [wkk] + hk, [pk], out=pb[:, 0:Tn], lhsT=wv[:, kc, blk * 128:(blk + 1) * 128], rhs=hT[:, kc, 0:Tn],
                          start=(kc == 0), stop=(kc == 7))
                    O("act" if blk % 2 else "dve", "activation" if blk % 2 else "tensor_copy", [pk], [("kloc", h)],
                      **(dict(out=kloc[:, h, 0:Tn], in_=pb[:, 0:Tn], func=AF.Copy) if blk % 2 else dict(out=kloc[:, h, 0:Tn], in_=pb[:, 0:Tn])))
                for s in range(nt):
                    pb, pk = pmbank()
                    for kc in range(8):
                        O("pe", "matmul", [wkk, ("hT", s)], [pk], out=pb[:, :], lhsT=hT[:, kc, s * 128:(s + 1) * 128], rhs=wv[:, kc, :],
                          start=(kc == 0), stop=(kc == 7))
                    O("act", "activation", [pk], [("yout", s, g)], out=yout[s][:, g * 512:(g + 1) * 512], in_=pb[:, :], func=AF.Copy)
            for s in range(nt):
                o = dma("pool", t["nk"][s * 128:(s + 1) * 128, :], yout[s][:], [("yout", s, 0), ("yout", s, 1)], [], ("yo", s))
                out_dmas.append(o)
            for g in range(2):
                wv, wkk = wload(("in", 6 + g))
                for s in range(nt):
                    pb, pk = pmbank()
                    for kc in range(8):
                        O("pe", "matmul", [wkk, ("hT", s)], [pk], out=pb[:, :], lhsT=hT[:, kc, s * 128:(s + 1) * 128], rhs=wv[:, kc, :],
                          start=(kc == 0), stop=(kc == 7))
                    O("act", "activation", [pk], [("yout", s, g)], out=yout[s][:, g * 512:(g + 1) * 512], in_=pb[:, :], func=AF.Copy)
                    O("dve", "tensor_copy", [pk], [("vloc", s)], out=vloc[:, s, g * 4:(g + 1) * 4, 0:128],
                      in_=pb[:, :].rearrange("p (h e) -> p h e", h=4))
            for s in range(nt):
                o = dma("pool", t["nv"][s * 128:(s + 1) * 128, :], yout[s][:], [("yout", s, 0), ("yout", s, 1)], [], ("yo", s))
                out_dmas.append(o)
        S.mark('m_kv')
        for g in range(4):
            pb, pk = pmbank()
            for s in range(nt):
                O("pe", "matmul", [("vg", s), "wsT"], [pk], out=pb[:, s * 128:(s + 1) * 128], lhsT=vg[:, s, g * 128:(g + 1) * 128], rhs=wsT[:, g, :],
                  start=True, stop=True)
            t4, k4 = tmp()
            for s in range(nt):
                O("dve", "scalar_tensor_tensor", [pk, "gsg", "bsb"], [(k4, s)], out=t4[:, s * 128:(s + 1) * 128], in0=pb[:, s * 128:(s + 1) * 128],
                  scalar=gsg[:, g:g + 1], in1=bsb[:, g, :], op0=ALU.mult, op1=ALU.add)
            O("dve", "scalar_tensor_tensor", [(k4, s) for s in range(nt)] + [("uT", g)], [("aT", g)], out=aT[:, g, 0:Tn], in0=t4[:, 0:Tn], scalar=0.25,
              in1=uT[:, g, 0:Tn], op0=ALU.mult, op1=ALU.mult)
        S.mark('m_gmlp')
        pmset[0] = [1]
        attention_all(Tn, nt, sample, NKT if sample else nt)
        for stg_ in range(5):
            attention_fin(NH - 1, nt, stg_)
        if not sample:
            attention_fin(NH - 2, nt, 5)
        attention_fin(NH - 1, nt, 5)
        S.mark('m_attn')
        pmset[0] = [1, 2, 7, 3, 4, 5, 6]
        bk = [("bT", h) for h in range(NH)]
        ak = [("aT", g) for g in range(4)]
        for g in range(2):
            wv, wkk = wload(("in", 8 + g))
            for blk in range(4):
                pb, pk = pmbank()
                for kc in range(8):
                    O("pe", "matmul", [wkk] + hk, [pk], out=pb[:, 0:Tn], lhsT=wv[:, kc, blk * 128:(blk + 1) * 128], rhs=hT[:, kc, 0:Tn],
                      start=(kc == 0), stop=(kc == 7))
                O("act", "activation", [pk], [("tga", blk)], out=tga[:, blk, 0:Tn], in_=pb[:, 0:Tn], func=AF.Tanh, scale=0.5)
            wv, wkk = wload(("in", 10 + g))
            for blk in range(4):
                pb, pk = pmbank()
                for kc in range(8):
                    O("pe", "matmul", [wkk] + hk, [pk], out=pb[:, 0:Tn], lhsT=wv[:, kc, blk * 128:(blk + 1) * 128], rhs=hT[:, kc, 0:Tn],
                      start=(kc == 0), stop=(kc == 7))
                O("act", "activation", [pk], [("tgb", blk)], out=tgb[:, blk, 0:Tn], in_=pb[:, 0:Tn], func=AF.Tanh, scale=0.5)
            wv, wkk = wload(("wa", g))
            for blk in range(4):
                pb, pk = pmbank()
                for kc in range(4):
                    O("pe", "matmul", [wkk] + ak, [pk], out=pb[:, 0:Tn], lhsT=wv[:, kc, blk * 128:(blk + 1) * 128], rhs=aT[:, kc, 0:Tn],
                      start=(kc == 0), stop=(kc == 3))
                O("dve", "scalar_tensor_tensor", [pk, ("tga", blk)], [("tga", blk)], out=tga[:, blk, 0:Tn], in0=tga[:, blk, 0:Tn], scalar=1.0,
                  in1=pb[:, 0:Tn], op0=ALU.add, op1=ALU.mult)
            wv, wkk = wload(("wb", g))
            for blk in range(4):
                pb, pk = pmbank()
                for kc in range(8):
                    O("pe", "matmul", [wkk] + bk, [pk], out=pb[:, 0:Tn], lhsT=wv[:, kc, blk * 128:(blk + 1) * 128], rhs=bT[:, kc, 0:Tn],
                      start=(kc == 0), stop=(kc == 7))
                O("dve", "scalar_tensor_tensor", [pk, ("tgb", blk)], [("tgb", blk)], out=tgb[:, blk, 0:Tn], in0=tgb[:, blk, 0:Tn], scalar=1.0,
                  in1=pb[:, 0:Tn], op0=ALU.add, op1=ALU.mult)
                O("dve", "tensor_tensor", [("tga", blk), ("tgb", blk)], [("mT", g * 4 + blk)], out=mT[:, g * 4 + blk, 0:Tn], in0=tga[:, blk, 0:Tn],
                  in1=tgb[:, blk, 0:Tn], op=ALU.add)
        S.mark('m_merge')
        mk = [("mT", j) for j in range(8)]
        for g in range(2):
            wv, wkk = wload(("wo", g))
            for s in range(nt):
                bi = 3 + 2 * s + g
                for kc in range(8):
                    O("pe", "matmul", [wkk] + mk, ["B%d" % bi], out=PB[bi][:, :], lhsT=mT[:, kc, s * 128:(s + 1) * 128], rhs=wv[:, kc, :],
                      start=(kc == 0), stop=(kc == 7))
        Gk = [("G", ci * 2 + 0, 0), ("G", ci * 2 + 0, 1)]
        for s in range(nt):
            post_norm_residual(s, xslot, xi, G[ci * 2 + 0], Gk, xslot[:, s, :], [("x", xi, s)], 48)
        S.mark('m_wo')
        hi = t["hi"]
        norm_T(xslot, xi, nt, scB[:, :, ci], modF[:, 24:32, ci], h2T[hi], ("h2T", hi), 64)
        tid = t["id"]
        O("pool", "tensor_copy", [(("h2T", hi), nt - 1)], ["HL"], out=HL[:, :, tid:tid + 1], in_=h2T[hi][:, :, Tn - 1:Tn])
        O("pool", "tensor_copy", [(("h2T", hi), 0)], ["HR"], out=HR[:, :, tid:tid + 1], in_=h2T[hi][:, :, 0:1])

    def post_norm_residual(s, xslot, xi, Gt, Gk, dst, dkeys, sbase):
        b0 = 3 + 2 * s
        c = sbase + 4 * s
        for g in range(2):
            j, jkk = jk()
            O("act", "activation", ["B%d" % (b0 + g)], [jkk, ("pss", c, g)], out=j[:, 0:512], in_=PB[b0 + g][:, :], func=AF.Square,
              accum_out=st[:, c + g:c + g + 1])
        O("dve", "tensor_tensor", [("pss", c, 0), ("pss", c, 1)], [("pss", c, 2)], out=st[:, c + 2:c + 3], in0=st[:, c:c + 1], in1=st[:, c + 1:c + 2],
          op=ALU.add)
        rstd_from(st[:, c + 2:c + 3], D, 0, st[:, c + 3:c + 4], [("pss", c, 2)], "pr%d" % c)
        for g in range(2):
            t4, k4 = tmp()
            O("dve", "scalar_tensor_tensor", ["B%d" % (b0 + g), "pr%d" % c, Gk[g]], [k4], out=t4[:, :], in0=PB[b0 + g][:, :],
              scalar=st[:, c + 3:c + 4], in1=Gt[:, g * 512:(g + 1) * 512], op0=ALU.mult, op1=ALU.mult)
            O("dve", "tensor_tensor", [k4, ("x", xi, s)], [dkeys[g if len(dkeys) > 1 else 0]],
              out=dst[:, g * 512:(g + 1) * 512], in0=t4[:, :], in1=xslot[:, s, g * 512:(g + 1) * 512], op=ALU.add)

    def ffn(t, hl_src, hr_src):
        Tn, nt, ci, xi, hi = t["T"], t["T"] // 128, t["ci"], t["xi"], t["hi"]
        xslot = xsl[xi]
        h2 = h2T[hi]
        hkey = ("h2T", hi)
        h2k = [(hkey, s) for s in range(nt)]
        halo = not (hl_src is None and hr_src is None)
        if halo:
            for col, src, hk_ in ((Tn, hl_src, (hkey, "hl")), (Tn + 1, hr_src, (hkey, "hr"))):
                if src is None:
                    O("pool", "memset", [], [hk_], ap=h2[:, :, col:col + 1], constant=0.0)
                else:
                    ap, keys, fc = src
                    if fc is None:
                        O("pool", "tensor_copy", keys, [hk_], out=h2[:, :, col:col + 1], in_=ap)
                    else:
                        O("pool", "tensor_scalar", keys + ["flg"], [hk_], out=h2[:, :, col:col + 1], in0=ap, scalar1=flg[:, fc:fc + 1], scalar2=None,
                          op0=ALU.mult)
        pbank = [1, 2]
        hbank = [7, 0]
        st_ = {"wu": None, "wdn": None}

        def ag(c):
            if c % 2 == 0:
                st_["wu"] = wload(("upp", c // 2))
            wu_v, wu_k = st_["wu"]
            ao = (c % 2) * 128
            go = 256 + (c % 2) * 128
            pb, pk = PB[pbank[c % 2]], "B%d" % pbank[c % 2]
            for kc in range(8):
                O("pe", "matmul", [wu_k] + h2k, [pk], out=pb[:, 0:Tn], lhsT=wu_v[:, kc, ao:ao + 128], rhs=h2[:, kc, 0:Tn], start=(kc == 0), stop=(kc == 7))
            if halo:
                hb, hbk = PB[hbank[c % 2]], "B%d" % hbank[c % 2]
                for kc in range(8):
                    O("pe", "matmul", [wu_k, (hkey, "hl"), (hkey, "hr")], [hbk], out=hb[:, 0:2], lhsT=wu_v[:, kc, ao:ao + 128], rhs=h2[:, kc, Tn:Tn + 2],
                      start=(kc == 0), stop=(kc == 7))
            for kc in range(8):
                O("pe", "matmul", [wu_k] + h2k, [pk], out=pb[:, Tn:2 * Tn], lhsT=wu_v[:, kc, go:go + 128], rhs=h2[:, kc, 0:Tn], start=(kc == 0), stop=(kc == 7))

        ag(0)
        for c in range(NFC):
            if c + 1 < NFC:
                ag(c + 1)
            pb, pk = PB[pbank[c % 2]], "B%d" % pbank[c % 2]
            hb, hbk = PB[hbank[c % 2]], "B%d" % hbank[c % 2]
            ac, ack = tmp()
            O("act", "activation", [pk, "cw"], [ack], out=ac[:, 0:Tn], in_=pb[:, 0:Tn], func=AF.Identity, scale=cw[:, c, 1:2], bias=cw[:, c, 3:4])
            O("dve", "scalar_tensor_tensor", [pk, "cw", ack], [ack], out=ac[:, 1:Tn], in0=pb[:, 0:Tn - 1], scalar=cw[:, c, 0:1], in1=ac[:, 1:Tn],
              op0=ALU.mult, op1=ALU.add)
            O("dve", "scalar_tensor_tensor", [pk, "cw", ack], [ack], out=ac[:, 0:Tn - 1], in0=pb[:, 1:Tn], scalar=cw[:, c, 2:3], in1=ac[:, 0:Tn - 1],
              op0=ALU.mult, op1=ALU.add)
            if halo:
                O("dve", "scalar_tensor_tensor", [hbk, "cw", ack], [ack], out=ac[:, 0:1], in0=hb[:, 0:1], scalar=cw[:, c, 0:1], in1=ac[:, 0:1],
                  op0=ALU.mult, op1=ALU.add)
                O("dve", "scalar_tensor_tensor", [hbk, "cw", ack], [ack], out=ac[:, Tn - 1:Tn], in0=hb[:, 1:2], scalar=cw[:, c, 2:3], in1=ac[:, Tn - 1:Tn],
                  op0=ALU.mult, op1=ALU.add)
            t0, k0 = tmp()
            O("act", "activation", [ack], [k0], out=t0[:, 0:Tn], in_=ac[:, 0:Tn], func=AF.Square, scale=GC0)
            O("dve", "scalar_tensor_tensor", [k0, ack], [(k0, "b")], out=t0[:, Tn:2 * Tn], in0=t0[:, 0:Tn], scalar=1.0, in1=ac[:, 0:Tn],
              op0=ALU.add, op1=ALU.mult)
            O("act", "activation", [(k0, "b")], [k0], out=t0[:, 0:Tn], in_=t0[:, Tn:2 * Tn], func=AF.Tanh, scale=GC1)
            O("dve", "scalar_tensor_tensor", [ack, pk], [(ack, "b")], out=ac[:, Tn:2 * Tn], in0=ac[:, 0:Tn], scalar=0.5, in1=pb[:, Tn:2 * Tn],
              op0=ALU.mult, op1=ALU.mult)
            zi = nxt("zT", 4)
            O("dve", "scalar_tensor_tensor", [k0, (ack, "b")], [("zT", zi)], out=zT[zi][:, 0:Tn], in0=t0[:, 0:Tn], scalar=1.0, in1=ac[:, Tn:2 * Tn],
              op0=ALU.add, op1=ALU.mult)
            if c % 4 == 0:
                st_["wdn"] = wload(("dn", c // 4))
            wv, wkk = st_["wdn"]
            for s in range(nt):
                for n in range(2):
                    bi = 3 + 2 * s + n
                    O("pe", "matmul", [wkk, ("zT", zi)], ["B%d" % bi], out=PB[bi][:, :], lhsT=zT[zi][:, s * 128:(s + 1) * 128],
                      rhs=wv[:, c % 4, n * 512:(n + 1) * 512], start=(c == 0), stop=(c == NFC - 1))
        Gk = [("G", ci * 2 + 1, 0), ("G", ci * 2 + 1, 1)]
        for s in range(nt):
            yi = nxt("yout", 2)
            post_norm_residual(s, xslot, xi, G[ci * 2 + 1], Gk, yout[yi], [("yout", yi, 0), ("yout", yi, 1)], 56)
            o = dma("pool", t["y"][s * 128:(s + 1) * 128, :], yout[yi][:], [("yout", yi, 0), ("yout", yi, 1)], [], ("yo", yi))
            out_dmas.append(o)

    def cache_to_scratch():
        for kt in range(4):
            xi = kt % 2
            dma("sp", xsl[xi][:, 0, :], ck[kt * 128:(kt + 1) * 128, :], [], [("x", xi, 0)], ("x", xi, 0))
            dma("sp", xsl[xi][:, 1, :], cv[kt * 128:(kt + 1) * 128, :], [], [("x", xi, 1)], ("x", xi, 1))
            xi2 = nxt("xn", 2)
            O("dve", "tensor_copy", [("x", xi, 0)], [("xn", xi2)], out=xn[xi2][:], in_=xsl[xi][:, 0, :])
            for h in range(8):
                O("pe", "transpose", [("xn", xi2), "ident"], ["B0"], out=B0[:, h, :], in_=xn[xi2][:, h * 128:(h + 1) * 128], identity=ident[:])
            O("dve", "tensor_copy", ["B0"], KLOC, out=kloc[:, :, (kt % 2) * 128:(kt % 2 + 1) * 128], in_=B0[:, :, :])
            O("act", "activation", [("x", xi, 1)], [("vloc", kt % 2)], out=vloc[:, kt % 2, :, 0:128], in_=xsl[xi][:, 1, :].rearrange("p (h e) -> p h e", h=8),
              func=AF.Copy)
            if kt % 2 == 1:
                k0 = (kt - 1) * 128
                dma("pool", Kscr[:, :, k0:k0 + 256].rearrange("h p k -> p h k"), kloc[:, :, 0:256], KLOC, [("Kscr", h) for h in range(8)], ("ksc", 0))
                for s in range(2):
                    dma("pool", Vscr[:, :, kt - 1 + s, :].rearrange("h p e -> p h e"), vloc[:, s, :, :], [("vloc", s)], [("Vscr", h) for h in range(8)], ("vsc", s))

    s1w = {}

    def s1_weights():
        names = [("in", 4), ("in", 5), ("in", 6), ("in", 7)]
        for name in names:
            if name not in converted:
                convert_into(name, nxt("ring", NRING))
        for i, name in enumerate(names):
            gi, nkc, ncols = groups[name]
            dma("sp", ring[i][:, :], Wscr[gi, :, :], [("Wscr", gi)], [("w", i)], ("w", i))
            s1w[name] = (ring[i][:, :].rearrange("p (k n) -> p k n", n=512), [("w", i)])

    s1ids = {}

    def s1_pre(j):
        xi = j % 2
        xslot = xsl[xi]
        for s in range(NT):
            dma("sp", xslot[:, s, :], xs[j * T + s * 128:j * T + (s + 1) * 128, :], [], [("x", xi, s)], ("x", xi, s))
        ri = nxt("rope", 2)
        dma("sp", rCt[ri][:, :, :], ropeCt[j * T:(j + 1) * T, :].rearrange("(s p) c -> p s c", p=128), [], [("rC", ri)], ("rC", ri))
        dma("sp", rSt[ri][:, :, :], ropeSt[j * T:(j + 1) * T, :].rearrange("(s p) c -> p s c", p=128), [], [("rS", ri)], ("rS", ri))
        s1ids[j] = (norm_stats(xslot, xi, NT, 0), ri)

    def s1_tile(j, nxt_pre):
        pmset[0] = [1, 2, 7, 3, 4, 5, 6]
        ids, ri = s1ids[j]
        norm_trans(ids, scA[:, :, 1], modF[:, 0:8, 1], hT, "hT")
        if nxt_pre is not None:
            s1_pre(nxt_pre)
        pmset[0] = [1, 2, 3, 4, 5, 6]
        kpend = []

        def k_flush():
            g_, s_, kb_, tqk_ = kpend.pop(0)
            tb_, tbk_ = TB[g_]
            for hh in range(4):
                O("pe", "transpose", [tqk_, "ident"], [tbk_], out=tb_[:, s_ * 4 + hh, :], in_=kb_[:, hh * 128:(hh + 1) * 128], identity=ident[:])
            if s_ % 2:
                O("act", "activation", [tbk_], [("kloc", g_ * 4 + hh) for hh in range(4)], out=kloc[:, g_ * 4:(g_ + 1) * 4, s_ * 128:(s_ + 1) * 128],
                  in_=tb_[:, s_ * 4:(s_ + 1) * 4, :], func=AF.Copy)
            else:
                O("dve", "tensor_copy", [tbk_], [("kloc", g_ * 4 + hh) for hh in range(4)], out=kloc[:, g_ * 4:(g_ + 1) * 4, s_ * 128:(s_ + 1) * 128],
                  in_=tb_[:, s_ * 4:(s_ + 1) * 4, :])

        for g in range(2):
            wv, wkl = s1w[("in", 4 + g)]
            for s in range(NT):
                pb, pk = pmbank()
                for kc in range(8):
                    O("pe", "matmul", wkl + [("hT", s)], [pk], out=pb[:, :], lhsT=hT[:, kc, s * 128:(s + 1) * 128], rhs=wv[:, kc, :], start=(kc == 0), stop=(kc == 7))
                tq, tqk = tmp()
                kb = tq[:, 0:256].bitcast(BF16)
                rope_tok(pb[:, :], pk, ri, s, kb, tqk)
                kpend.append((g, s, kb, tqk))
                if len(kpend) > 1:
                    k_flush()
        k0 = 512 + j * T
        for g in range(2):
            wv, wkl = s1w[("in", 6 + g)]
            if g == 1:
                while kpend:
                    k_flush()
                dma("pool", Kscr[:, :, k0:k0 + T].rearrange("h p k -> p h k"), kloc[:, :, :], KLOC, [("Kscr", h) for h in range(8)], ("ksc", 0))
            for s in range(NT):
                pb, pk = pmbank()
                for kc in range(8):
                    O("pe", "matmul", wkl + [("hT", s)], [pk], out=pb[:, :], lhsT=hT[:, kc, s * 128:(s + 1) * 128], rhs=wv[:, kc, :], start=(kc == 0), stop=(kc == 7))
                O("act", "activation", [pk], [("vloc", s)], out=vloc[:, s, g * 4:(g + 1) * 4, 0:128], in_=pb[:, :].rearrange("p (h e) -> p h e", h=4), func=AF.Copy)
        for s in range(NT):
            dma("pool", Vscr[:, :, 4 + j * NT + s, :].rearrange("h p e -> p h e"), vloc[:, s, :, :], [("vloc", s)], [("Vscr", h) for h in range(8)], ("vsc", s))

    phase0()
    ptiles = [dict(id=p, kind="p", T=T, ci=0, xi=p % 3, hi=p % 2, src=xp[p * T:(p + 1) * T, :], y=yp[p * T:(p + 1) * T, :],
                   nk=nk[p * T:(p + 1) * T, :], nv=nv[p * T:(p + 1) * T, :]) for p in range(NP_TILES)]
    if NP_TILES:
        mixer_pre(ptiles[0])
    for p in range(NP_TILES):
        mixer(ptiles[p])
        S.mark('mixer_done')
        if p + 1 < NP_TILES:
            mixer_pre(ptiles[p + 1])
        ffn(ptiles[p], None, None)
        S.mark('ffn_done')
    if DO_SAMPLE:
        cache_to_scratch()
        s1_weights()
        NS1 = 4096 // T
        s1_pre(0)
        for j in range(NS1):
            s1_tile(j, j + 1 if j + 1 < NS1 else None)
        S.mark('s1_done')
        NB = 15
        tn = dict(id=NB, kind="nb", T=128, ci=1, xi=2, hi=0, src=xs[2048:2176, :], roff=2048)
        tiles = []
        for i in range(NS_TILES):
            tiles.append(dict(id=i, kind="s", T=T, ci=1, xi=i % 3, hi=(i + 1) % 2, src=xs[i * T:(i + 1) * T, :], roff=i * T,
                              y=ys[i * T:(i + 1) * T, :]))

        def halo(i):
            if i == 0:
                hl = (HL[:, :, NB:NB + 1], ["HL"], 0)
            else:
                hl = (HL[:, :, i - 1:i], ["HL"], None)
            if i == NS_TILES - 1:
                hr = (HR[:, :, NB:NB + 1], ["HR"], 1)
            else:
                hr = (HR[:, :, i + 1:i + 2], ["HR"], None)
            return hl, hr

        mixer_pre(tn)
        mixer(tn)
        mixer_pre(tiles[0])
        for i in range(NS_TILES):
            mixer(tiles[i])
            if i + 1 < NS_TILES:
                mixer_pre(tiles[i + 1])
            if i >= 1:
                ffn(tiles[i - 1], *halo(i - 1))
        ffn(tiles[NS_TILES - 1], *halo(NS_TILES - 1))

    S.limit = 1000000000
    fin = S.op("sp", lambda e: None, [], [])
    fin.deps = [o for o in out_dmas if o.dma is not None]
    if os.environ.get("MK_MARK"):
        print("SBUF free bytes/partition", nc.sbuf_bytes_remaining)
    with ExitStack() as stack:
        S.emit(stack)
    return nc


def _rope_tables():
    n_freq = 16
    inv = (10000.0 ** (-np.arange(n_freq, dtype=np.float32) / n_freq)).astype(np.float32)
    tok = np.arange(4096)
    row = (tok // 64).astype(np.float32)
    col = (tok % 64).astype(np.float32)
    ang = np.concatenate([row[:, None] * inv, col[:, None] * inv], axis=-1).astype(np.float32)
    cos = np.cos(ang).astype(np.float32)
    sin = np.sin(ang).astype(np.float32)
    d = np.arange(64)
    C = cos[:, d % 32]
    Sg = sin[:, d % 32] * np.where(d < 32, -1.0, 1.0)[None, :]
    return np.ascontiguousarray(C, np.float32), np.ascontiguousarray(Sg, np.float32)


def _fm(v, nchunk):
    return np.ascontiguousarray(np.asarray(v, np.float32).reshape(nchunk, 128).T)


_NC_CACHE = {}


def kernel(x_prompt, x_sample, c, cache_k, cache_v, c_ctx, w_ada, b_ada, g_pre_mix, g_post_mix,
           g_pre_ffn, g_post_ffn, w_in, g_sgu, w_s, b_s, lam_q1, lam_k1, lam_q2, lam_k2, g_subln,
           w_a, w_b, w_o, w_up, conv_w, conv_b, w_down):
    f = lambda a: np.ascontiguousarray(np.asarray(a, dtype=np.float32))
    x_prompt, x_sample, c, cache_k, cache_v, c_ctx = map(f, (x_prompt, x_sample, c, cache_k, cache_v, c_ctx))
    w_in0 = f(w_in)[0]
    C, Sg = _rope_tables()
    shared = {
        "w_ada": f(w_ada)[0], "b_adaF": _fm(f(b_ada)[0], 48), "b_ada": f(b_ada)[0],
        "gpreF": np.ascontiguousarray(np.stack([_fm(f(g_pre_mix)[0], 8), _fm(f(g_pre_ffn)[0], 8)], axis=-1)),
        "gpost": np.ascontiguousarray(np.stack([f(g_post_mix)[0], f(g_post_ffn)[0]], axis=0)),
        "w_in": w_in0,
        "gsguF": _fm(f(g_sgu)[0], 4),
        "w_sT": np.ascontiguousarray(f(w_s)[0].transpose(2, 0, 1)),
        "b_s": np.ascontiguousarray(f(b_s)[0].reshape(512)),
        "lamv": np.ascontiguousarray(np.stack([f(lam_q1)[0], f(lam_k1)[0], f(lam_q2)[0], f(lam_k2)[0]], axis=0)),
        "gsub": f(g_subln)[0],
        "w_a": f(w_a)[0], "w_b": f(w_b)[0], "w_o": f(w_o)[0], "w_up": f(w_up)[0],
        "convF": np.ascontiguousarray(np.concatenate([f(conv_w)[0], f(conv_b)], axis=0).T.reshape(NFC, 128, 4).transpose(1, 0, 2)),
        "w_down": f(w_down)[0],
        "ident": np.eye(128).astype(ml_dtypes.bfloat16),
    }
    in_maps = []
    for core in range(8):
        b, half = core // 2, core % 2
        if half == 0:
            order = np.concatenate([np.arange(0, 2048), np.arange(2048, 2176), np.arange(2176, 4096)])
            flags = np.tile(np.array([[0.0, 1.0]], np.float32), (128, 1))
        else:
            order = np.concatenate([np.arange(2048, 4096), np.arange(1920, 2048), np.arange(0, 1920)])
            flags = np.tile(np.array([[1.0, 0.0]], np.float32), (128, 1))
        m = dict(shared)
        m["xp"] = np.ascontiguousarray(x_prompt[4 * core:4 * core + 4].reshape(4 * T, D))
        m["xs"] = np.ascontiguousarray(x_sample[b][order])
        m["cvec"] = np.ascontiguousarray(np.stack([_fm(c_ctx, 8), _fm(c[b], 8)], axis=-1))
        m["ck"] = np.ascontiguousarray(cache_k[b, 0].reshape(512, D))
        m["cv"] = np.ascontiguousarray(cache_v[b, 0].reshape(512, D))
        m["ropeCt"] = np.ascontiguousarray(C[order])
        m["ropeSt"] = np.ascontiguousarray(Sg[order])
        m["flags"] = flags
        in_maps.append(m)
    if "nc" not in _NC_CACHE:
        _NC_CACHE["nc"] = build_program()
    nc = _NC_CACHE["nc"]
    res = run_bass_kernel_spmd(nc, in_maps, core_ids=list(range(8)))
    r = res.results
    y_prompt = np.concatenate([r[i]["yp"].reshape(4, T, D) for i in range(8)], axis=0).astype(np.float32)
    y_sample = np.stack([np.concatenate([r[2 * b]["ys"], r[2 * b + 1]["ys"]], axis=0) for b in range(4)], axis=0).astype(np.float32)
    new_k = np.concatenate([r[i]["nk"].reshape(4, 1, T, NH, 128) for i in range(8)], axis=0).astype(np.float32)
    new_v = np.concatenate([r[i]["nv"].reshape(4, 1, T, NH, 128) for i in range(8)], axis=0).astype(np.float32)
    return (y_prompt, y_sample, new_k, new_v)
```
